# Optimizing a Trainium2 kernel written in Bass

```python
import jax, jax.numpy as jnp
from jax import lax
import numpy as np

D_MODEL = 2048
BATCH = 2
SEQ = 16384
DEPTH = 1
DEC_BATCH = 4
DEC_SEQ = 2048
PAST_LEN = 128

N_ATTN_HEADS = 8
QK_NOPE_DIM = 128
QK_ROPE_DIM = 64
QK_HEAD_DIM = QK_NOPE_DIM + QK_ROPE_DIM
V_HEAD_DIM = 128
ATTN_WIDTH = N_ATTN_HEADS * V_HEAD_DIM
Q_LORA_RANK = 512
KV_LORA_RANK = 256
ROPE_BASE = 10000.0
Q_BLOCK = 128
LRU_WIDTH = D_MODEL - ATTN_WIDTH
LRU_BLOCKS = 8
LRU_BLOCK_DIM = LRU_WIDTH // LRU_BLOCKS
CONV_WIDTH = 4
CONV_PAD_LEFT = 1
CONV_PAD_RIGHT = CONV_WIDTH - 1 - CONV_PAD_LEFT
LRU_C = 8.0
MIX_WIDTH = ATTN_WIDTH + LRU_WIDTH
D_FF = -(-8 * D_MODEL // (3 * 256)) * 256
EPS = 1e-6
OFF_Q = Q_LORA_RANK
OFF_KV = OFF_Q + KV_LORA_RANK
OFF_PE = OFF_KV + QK_ROPE_DIM
OFF_LX = OFF_PE + LRU_WIDTH
IN_COLS = OFF_LX + LRU_WIDTH

kernel_name = "hymba_mla_rglru_encoder"


def rms_norm(x, g):
    xf = x.astype(jnp.float32)
    y = xf * lax.rsqrt(jnp.mean(xf * xf, axis=-1, keepdims=True) + EPS)
    return (y * g.astype(jnp.float32)).astype(x.dtype)


def rope(x, positions):
    half = QK_ROPE_DIM // 2
    inv_freq = ROPE_BASE ** (-2.0 * jnp.arange(half, dtype=jnp.float32) / QK_ROPE_DIM)
    ang = positions[:, None] * inv_freq[None, :]
    cos = jnp.cos(ang)[:, None, :]
    sin = jnp.sin(ang)[:, None, :]
    xf = x.astype(jnp.float32)
    x1, x2 = xf[..., :half], xf[..., half:]
    out = jnp.concatenate([x1 * cos - x2 * sin, x2 * cos + x1 * sin], axis=-1)
    return out.astype(x.dtype)


def block_attention(q, k, v):
    B, S, H, Dk = q.shape
    nb = S // Q_BLOCK
    qb = q.reshape(B, nb, Q_BLOCK, H, Dk).transpose(1, 0, 2, 3, 4)
    scale = QK_HEAD_DIM ** -0.5

    def one_block(qi):
        s = jnp.einsum('bqhd,bkhd->bhqk', qi, k, preferred_element_type=jnp.float32) * scale
        p = jax.nn.softmax(s, axis=-1)
        return jnp.einsum('bhqk,bkhd->bqhd', p.astype(v.dtype), v)

    o = lax.map(one_block, qb)
    return o.transpose(1, 0, 2, 3, 4).reshape(B, S, H, V_HEAD_DIM)


def mla_mixer(c_q, c_kv, k_pe, positions, q_a_norm_g, w_uq, kv_a_norm_g, w_ukv,
              q_norm_g, k_norm_g):
    B, S, _ = c_q.shape
    q = (rms_norm(c_q, q_a_norm_g) @ w_uq).reshape(B, S, N_ATTN_HEADS, QK_HEAD_DIM)
    kv = (rms_norm(c_kv, kv_a_norm_g) @ w_ukv).reshape(B, S, N_ATTN_HEADS, QK_NOPE_DIM + V_HEAD_DIM)
    k_nope, v = kv[..., :QK_NOPE_DIM], kv[..., QK_NOPE_DIM:]
    k_pe_h = jnp.broadcast_to(k_pe[:, :, None, :], (B, S, N_ATTN_HEADS, QK_ROPE_DIM))
    k = jnp.concatenate([k_nope, k_pe_h], axis=-1)
    q = rms_norm(q, q_norm_g)
    k = rms_norm(k, k_norm_g)
    q = jnp.concatenate([q[..., :QK_NOPE_DIM], rope(q[..., QK_NOPE_DIM:], positions)], axis=-1)
    k = jnp.concatenate([k[..., :QK_NOPE_DIM], rope(k[..., QK_NOPE_DIM:], positions)], axis=-1)
    o = block_attention(q, k, v)
    return o.reshape(B, S, ATTN_WIDTH)


def centred_depthwise_conv(x, w, b):
    S = x.shape[1]
    xp = jnp.pad(x, ((0, 0), (CONV_PAD_LEFT, CONV_PAD_RIGHT), (0, 0)))
    y = b
    for t in range(CONV_WIDTH):
        y = y + xp[:, t:t + S] * w[t]
    return y


def block_diag(x, w, b):
    B, S, _ = x.shape
    xb = x.reshape(B, S, LRU_BLOCKS, LRU_BLOCK_DIM)
    return (jnp.einsum('bsnc,ncd->bsnd', xb, w) + b).reshape(B, S, LRU_WIDTH)


def linear_scan(a, bx, reverse):
    def combine(l, r):
        a_l, b_l = l
        a_r, b_r = r
        return a_l * a_r, a_r * b_l + b_r
    _, h = lax.associative_scan(combine, (a, bx), axis=1, reverse=reverse)
    return h


def rg_lru_direction(xf, w_r, b_r, w_i, b_i, lam, reverse):
    f32 = jnp.float32
    r = jax.nn.sigmoid(block_diag(xf, w_r.astype(f32), b_r.astype(f32)))
    i = jax.nn.sigmoid(block_diag(xf, w_i.astype(f32), b_i.astype(f32)))
    log_a = -LRU_C * r * jax.nn.softplus(-lam.astype(f32))
    a = jnp.exp(log_a)
    mult = jnp.sqrt(-jnp.expm1(2.0 * log_a))
    return linear_scan(a, mult * (i * xf), reverse)


def rglru_mixer(u_x, u_gate, conv_w, conv_b, w_rg_r, b_rg_r, w_rg_i, b_rg_i, lru_lambda):
    xc = centred_depthwise_conv(u_x, conv_w, conv_b).astype(jnp.float32)
    h = (rg_lru_direction(xc, w_rg_r[0], b_rg_r[0], w_rg_i[0], b_rg_i[0], lru_lambda[0], False)
         + rg_lru_direction(xc, w_rg_r[1], b_rg_r[1], w_rg_i[1], b_rg_i[1], lru_lambda[1], True))
    return (h * jax.nn.gelu(u_gate.astype(jnp.float32))).astype(u_x.dtype)


def encoder_layer(x, attn_norm_g, w_in, q_a_norm_g, w_uq, kv_a_norm_g, w_ukv, q_norm_g,
                  k_norm_g, conv_w, conv_b, w_rg_r, b_rg_r, w_rg_i, b_rg_i, lru_lambda,
                  attn_out_norm_g, lru_out_norm_g, w_out, ffn_norm_g, w_gate, w_up, w_down):
    S = x.shape[1]
    positions = jnp.arange(S, dtype=jnp.float32)
    h = rms_norm(x, attn_norm_g)
    u = h @ w_in
    c_q = u[..., :OFF_Q]
    c_kv = u[..., OFF_Q:OFF_KV]
    k_pe = u[..., OFF_KV:OFF_PE]
    u_x = u[..., OFF_PE:OFF_LX]
    u_gate = u[..., OFF_LX:]
    attn = mla_mixer(c_q, c_kv, k_pe, positions, q_a_norm_g, w_uq, kv_a_norm_g, w_ukv,
                     q_norm_g, k_norm_g)
    lru = rglru_mixer(u_x, u_gate, conv_w, conv_b, w_rg_r, b_rg_r, w_rg_i, b_rg_i,
                      lru_lambda)
    mixed = jnp.concatenate([rms_norm(attn, attn_out_norm_g), rms_norm(lru, lru_out_norm_g)], axis=-1)
    x = x + mixed @ w_out
    h = rms_norm(x, ffn_norm_g)
    x = x + (jax.nn.silu(h @ w_gate) * (h @ w_up)) @ w_down
    return x


def setup_inputs(seed: int = 0) -> dict:
    key = jax.random.key(seed)
    ks = jax.random.split(key, 32)
    f32 = jnp.float32

    def dense(k, shape, fan_in):
        return jax.random.normal(k, shape, f32) * fan_in ** -0.5

    def gain(k, n):
        return 1.0 + 0.01 * jax.random.normal(k, (DEPTH, n), f32)

    def small(k, shape):
        return 0.01 * jax.random.normal(k, shape, f32)

    u = jax.random.uniform(ks[18], (DEPTH, 2, LRU_WIDTH), f32, minval=0.9, maxval=0.999)
    return {
        "x_prompt": jax.random.normal(ks[0], (BATCH, SEQ, D_MODEL), f32),
        "x_sample": jax.random.normal(ks[1], (DEC_BATCH, DEC_SEQ, D_MODEL), f32),
        "attn_norm_g": gain(ks[2], D_MODEL),
        "w_in": dense(ks[3], (DEPTH, D_MODEL, IN_COLS), D_MODEL),
        "q_a_norm_g": gain(ks[4], Q_LORA_RANK),
        "w_uq": dense(ks[5], (DEPTH, Q_LORA_RANK, N_ATTN_HEADS * QK_HEAD_DIM), Q_LORA_RANK),
        "kv_a_norm_g": gain(ks[6], KV_LORA_RANK),
        "w_ukv": dense(ks[7], (DEPTH, KV_LORA_RANK, N_ATTN_HEADS * (QK_NOPE_DIM + V_HEAD_DIM)), KV_LORA_RANK),
        "q_norm_g": gain(ks[8], QK_HEAD_DIM),
        "k_norm_g": gain(ks[9], QK_HEAD_DIM),
        "conv_w": dense(ks[10], (DEPTH, CONV_WIDTH, LRU_WIDTH), CONV_WIDTH),
        "conv_b": small(ks[11], (DEPTH, LRU_WIDTH)),
        "w_rg_r": dense(ks[12], (DEPTH, 2, LRU_BLOCKS, LRU_BLOCK_DIM, LRU_BLOCK_DIM), LRU_BLOCK_DIM),
        "b_rg_r": small(ks[13], (DEPTH, 2, LRU_BLOCKS, LRU_BLOCK_DIM)),
        "w_rg_i": dense(ks[14], (DEPTH, 2, LRU_BLOCKS, LRU_BLOCK_DIM, LRU_BLOCK_DIM), LRU_BLOCK_DIM),
        "b_rg_i": small(ks[15], (DEPTH, 2, LRU_BLOCKS, LRU_BLOCK_DIM)),
        "lru_lambda": jnp.log(u) - jnp.log1p(-u),
        "attn_out_norm_g": gain(ks[16], ATTN_WIDTH),
        "lru_out_norm_g": gain(ks[17], LRU_WIDTH),
        "w_out": dense(ks[19], (DEPTH, MIX_WIDTH, D_MODEL), MIX_WIDTH),
        "ffn_norm_g": gain(ks[20], D_MODEL),
        "w_gate": dense(ks[21], (DEPTH, D_MODEL, D_FF), D_MODEL),
        "w_up": dense(ks[22], (DEPTH, D_MODEL, D_FF), D_MODEL),
        "w_down": dense(ks[23], (DEPTH, D_FF, D_MODEL), D_FF),
    }


def reference(x_prompt, x_sample, attn_norm_g, w_in, q_a_norm_g, w_uq, kv_a_norm_g, w_ukv,
              q_norm_g, k_norm_g, conv_w, conv_b, w_rg_r, b_rg_r, w_rg_i, b_rg_i, lru_lambda,
              attn_out_norm_g, lru_out_norm_g, w_out, ffn_norm_g, w_gate, w_up, w_down):
    y_prompt = x_prompt
    y_sample = x_sample
    for l in range(DEPTH):
        p = (attn_norm_g[l], w_in[l], q_a_norm_g[l], w_uq[l], kv_a_norm_g[l], w_ukv[l],
             q_norm_g[l], k_norm_g[l], conv_w[l], conv_b[l], w_rg_r[l], b_rg_r[l],
             w_rg_i[l], b_rg_i[l], lru_lambda[l], attn_out_norm_g[l], lru_out_norm_g[l],
             w_out[l], ffn_norm_g[l], w_gate[l], w_up[l], w_down[l])
        y_prompt = encoder_layer(y_prompt, *p)
        y_sample = encoder_layer(y_sample, *p)
    return (y_prompt, y_sample)
```

```python
import os
import numpy as np
import concourse.bass as bass
import concourse.mybir as mybir
from concourse.bass_utils import run_bass_kernel_spmd

F32 = mybir.dt.float32
BF16 = mybir.dt.bfloat16
ALU = mybir.AluOpType
AF = mybir.ActivationFunctionType

D = 2048
NH = 8
DFF = 5632
NFC = DFF // 128
EPS = 1e-6
CH = 512
NQD = 16

PC = {}
_o = 0
for _n, _w in [("g_attn", 16), ("g_ffn", 16), ("g_qa", 4), ("g_kva", 2), ("g_qn", 1), ("g_qp", 1), ("g_qps", 1),
               ("g_kn", 1), ("g_kp", 1), ("g_kps", 1), ("conv_w", 32), ("conv_b", 8), ("b_r", 16), ("b_i", 16),
               ("lam", 16), ("g_ao", 8), ("g_lo", 8)]:
    PC[_n] = _o
    _o += _w
NPAR = _o


class Buf:
    __slots__ = ("name", "w", "r")

    def __init__(self, name=""):
        self.name = name
        self.w = None
        self.r = {}


class Eng:
    def __init__(self, name, is_dma=False):
        self.name = name
        self.is_dma = is_dma
        self.ops = []
        self.seen = {}
        self.count = 0
        self.sem = None
        self.qsems = []
        self.qcnt = []
        self.rr = 0
        self.pending = []


class Prog:
    def __init__(self, nc):
        self.nc = nc
        self.sems = []
        self.eng = {}
        for n in ("pe", "act", "dve", "pool"):
            e = Eng(n)
            e.sem = self._newsem(n)
            self.eng[n] = e
        for n in ("sp", "pq"):
            e = Eng(n, True)
            e.qsems = [self._newsem(f"{n}{i}") for i in range(NQD)]
            e.qcnt = [0] * NQD
            self.eng[n] = e
        self.n_ops = 0

    def _newsem(self, name):
        self.sems.append(self.nc.alloc_semaphore("s_" + name))
        return len(self.sems) - 1

    def _stream(self, en):
        return self.eng["pool"] if en == "pq" else self.eng[en]

    def _waits(self, en, reads, writes, extra=()):
        st = self._stream(en)
        need = {}

        def add(tok):
            if tok is None:
                return
            s, v = tok
            if need.get(s, 0) < v:
                need[s] = v
        for b in reads:
            add(b.w)
        for b in writes:
            add(b.w)
            for s, v in b.r.items():
                add((s, v))
        for t in extra:
            add(t)
        out = []
        for s, v in need.items():
            if en == "pe" and s == self.eng["pe"].sem:
                continue
            if st.seen.get(s, 0) >= v:
                continue
            st.seen[s] = v
            out.append((s, v))
        return out

    def _commit(self, tok, reads, writes):
        s, v = tok
        for b in reads:
            if b.r.get(s, 0) < v:
                b.r[s] = v
        for b in writes:
            b.w = tok
            b.r = {}

    def op(self, en, fn, R=(), W=(), inc=True):
        e = self.eng[en]
        waits = self._waits(en, R, W)
        self.n_ops += 1
        if inc:
            e.count += 1
            tok = (e.sem, e.count)
            e.ops.append((waits, fn, (e.sem, 1)))
            self._commit(tok, R, W)
            for (r, w) in e.pending:
                self._commit(tok, r, w)
            e.pending = []
        else:
            e.ops.append((waits, fn, None))
            e.pending.append((tuple(R), tuple(W)))

    def dma(self, q, out, in_, R=(), W=(), slow=False):
        e = self.eng[q]
        i = e.rr % NQD
        e.rr += 1
        extra = []
        if e.qcnt[i] > 0:
            extra.append((e.qsems[i], 16 * e.qcnt[i]))
        waits = self._waits(q, R, W, extra)
        e.qcnt[i] += 1
        tok = (e.qsems[i], 16 * e.qcnt[i])
        st = self._stream(q)
        if slow:
            st.ops.append((waits, (lambda eng, o=out, s=in_: eng.dma_start(out=o, in_=s, allow_slow_non_contiguous=True)),
                           (e.qsems[i], 16)))
        else:
            st.ops.append((waits, (lambda eng, o=out, s=in_: eng.dma_start(out=o, in_=s)), (e.qsems[i], 16)))
        self._commit(tok, R, W)
        self.n_ops += 1

    def barrier(self):
        toks = []
        for n in ("pe", "act", "dve", "pool"):
            e = self.eng[n]
            assert not e.pending
            if e.count:
                toks.append((e.sem, e.count))
        for n in ("sp", "pq"):
            e = self.eng[n]
            for i in range(NQD):
                if e.qcnt[i]:
                    toks.append((e.qsems[i], 16 * e.qcnt[i]))
        for n in ("pe", "act", "dve", "pool", "sp"):
            st = self.eng[n]
            ws = []
            for s, v in toks:
                if st.seen.get(s, 0) >= v:
                    continue
                if n != "sp" and s == st.sem:
                    pass
                st.seen[s] = v
                ws.append((s, v))
            if ws:
                st.ops.append((ws, None, None))

    def replay(self, en, e):
        for waits, fn, inc in self.eng[en].ops:
            for s, v in waits:
                e.wait_ge(self.sems[s], v)
            if fn is None:
                continue
            ins = fn(e)
            if inc is not None:
                ins.then_inc(self.sems[inc[0]], inc[1])

    def mm(self, out, lhsT, rhs, start, stop, R, W, inc=None):
        if inc is None:
            inc = stop
        self.op("pe", lambda e: e.matmul(out, lhsT=lhsT, rhs=rhs, start=start, stop=stop), R, W, inc)

    def tr(self, out, in_, ident, R, W, inc=True):
        self.op("pe", lambda e: e.transpose(out, in_, ident), R, W, inc)

    def act(self, out, in_, func, R, W, bias=None, scale=None, accum=None, en="act"):
        kw = {}
        if bias is not None:
            kw["bias"] = bias
        if scale is not None:
            kw["scale"] = scale
        if accum is not None:
            kw["accum_out"] = accum
        self.op(en, lambda e: e.activation(out=out, in_=in_, func=func, **kw), R, W)

    def tt(self, en, out, in0, in1, op, R, W):
        self.op(en, lambda e: e.tensor_tensor(out=out, in0=in0, in1=in1, op=op), R, W)

    def ts(self, en, out, in0, s1, s2, op0, op1, R, W):
        if s2 is None:
            self.op(en, lambda e: e.tensor_scalar(out=out, in0=in0, scalar1=s1, scalar2=None, op0=op0), R, W)
        else:
            self.op(en, lambda e: e.tensor_scalar(out=out, in0=in0, scalar1=s1, scalar2=s2, op0=op0, op1=op1), R, W)

    def stt(self, en, out, in0, scalar, in1, op0, op1, R, W):
        self.op(en, lambda e: e.scalar_tensor_tensor(out=out, in0=in0, scalar=scalar, in1=in1, op0=op0, op1=op1), R, W)

    def copy(self, en, out, in_, R, W):
        if en == "act":
            self.op(en, lambda e: e.activation(out=out, in_=in_, func=AF.Copy), R, W)
        else:
            self.op(en, lambda e: e.tensor_copy(out=out, in_=in_), R, W)

    def recip(self, out, in_, R, W):
        self.op("dve", lambda e: e.reciprocal(out=out, in_=in_), R, W)

    def scan(self, out, d0, d1, init, R, W):
        self.op("dve", lambda e: e.tensor_tensor_scan(out=out, data0=d0, data1=d1, initial=init,
                                                        op0=ALU.mult, op1=ALU.add), R, W)

    def memset(self, en, ap, val, W):
        self.op(en, lambda e: e.memset(ap, val), (), W)


class Arena:
    def __init__(self, nc, nbytes):
        self.t = nc.alloc_sbuf_tensor("arena", [128, nbytes // 4], F32).ap()
        self.top = 0
        self.cap = nbytes

    def f32(self, n, parts=128):
        off = self.top
        self.top += ((n * 4 + 31) // 32) * 32
        assert self.top <= self.cap, f"SBUF overflow {self.top}"
        return self.t[0:parts, off // 4: off // 4 + n]

    def bf16(self, n, parts=128):
        assert n % 2 == 0
        off = self.top
        self.top += ((n * 2 + 31) // 32) * 32
        assert self.top <= self.cap, f"SBUF overflow {self.top}"
        return self.t[0:parts, off // 4: off // 4 + n // 2].bitcast(BF16)


class Pool:
    def __init__(self, tiles, track=False):
        self.tiles = tiles
        self.bufs = [Buf() for _ in tiles]
        self.i = 0
        self.track = track
        self.held = [False] * len(tiles)

    def get(self):
        n = len(self.tiles)
        for _ in range(n):
            k = self.i % n
            self.i += 1
            if not self.held[k]:
                break
        else:
            raise RuntimeError("pool exhausted")
        if self.track:
            self.held[k] = True
        return self.tiles[k], self.bufs[k]

    def rel(self, buf):
        for k, b in enumerate(self.bufs):
            if b is buf:
                self.held[k] = False
                return
        raise KeyError("buf not in pool")


def build(cfg):
    nc = bass.Bass("TRN2", target_bir_lowering=False)
    P = Prog(nc)
    jobs = cfg["jobs"]
    dbg = cfg.get("debug", False)
    stages = cfg.get("stages", "01234")

    def din(name, shape, dt=F32):
        return nc.dram_tensor(name, list(shape), dt, kind="ExternalInput").ap()

    def dscr(name, shape, dt):
        kind = "ExternalOutput" if dbg else "Internal"
        return nc.dram_tensor(name, list(shape), dt, kind=kind).ap()

    w_in = din("w_in", [D, 2880])
    w_pesw = din("w_pesw", [D, 64])
    w_uq = din("w_uq", [512, 1536])
    w_uqsw = din("w_uqsw", [512, 512])
    w_ukv = din("w_ukv", [256, 2048])
    w_rg = din("w_rg", [2, 2, 8, 128, 128])
    w_out = din("w_out", [D, D])
    w_gate = din("w_gate", [D, DFF])
    w_up = din("w_up", [D, DFF])
    w_down = din("w_down", [DFF, D])
    ident_d = din("ident", [128, 128])
    params_d = din("params", [128, NPAR])
    msk_d = din("msk", [128, 8])

    J = {}
    for (jn, S, NOWN) in jobs:
        j = {"S": S, "NOWN": NOWN}
        j["x"] = din("x" + jn, [S, D])
        j["rope"] = din("rope" + jn, [2, 64, S])
        j["y"] = nc.dram_tensor("y" + jn, [NOWN, D], F32, kind="ExternalOutput").ap()
        j["Kn"] = dscr("Kn" + jn, [NH, 128, S], BF16)
        j["Kp"] = dscr("Kp" + jn, [NH, 64, S], BF16)
        j["V"] = dscr("V" + jn, [NH, 128, S // 128, 128], BF16)
        j["Qn"] = dscr("Qn" + jn, [NH, 128, NOWN], BF16)
        j["Qp"] = dscr("Qp" + jn, [NH, 64, NOWN], BF16)
        j["U"] = dscr("U" + jn, [8, 128, S + 8], F32)
        j["G"] = dscr("G" + jn, [8, 128, NOWN], F32)
        j["Hf"] = dscr("Hf" + jn, [8, 128, NOWN], F32)
        j["MIX"] = dscr("MIX" + jn, [16, 128, NOWN], F32)
        J[jn] = j
    NSL_IN = 6
    NSL = NSL_IN + 4 + 22 + 16
    wslab = nc.dram_tensor("wslab", [NSL, 128, 8192], BF16, kind="Internal").ap()
    SL_OUT = NSL_IN
    SL_GU = SL_OUT + 4
    SL_DN = SL_GU + 22

    A = Arena(nc, 212480)
    PS = nc.alloc_psum_tensor("ps", [128, 4096], F32).ap()
    bank = [PS[:, b * 512:(b + 1) * 512] for b in range(8)]
    bankb = [Buf(f"bank{b}") for b in range(8)]
    PSB = PS[:, 0:1024].bitcast(BF16)

    ident = A.bf16(128)
    ones = A.bf16(128)
    par = A.f32(NPAR)
    msk = A.f32(8)
    clam = A.f32(16)
    gqs = A.f32(3)
    epst = A.f32(1)
    one_t = A.f32(1)
    cB = Buf("consts")
    slabB = [Buf(f"slab{i}") for i in range(NSL)]
    wrg_c = A.bf16(2 * 2 * 8 * 128).rearrange("p (g d n o) -> p g d n o", g=2, d=2, n=8)
    wB2_c = Buf("wrg")
    carry_c = A.f32(16)
    for g_ in range(2):
        for d__ in range(2):
            P.dma("pq", wrg_c[:, g_, d__], w_rg[g_, d__].rearrange("n c o -> c n o"), (), (wB2_c,))

    P.dma("pq", ident, ident_d, (), (cB,))
    P.dma("sp", par, params_d, (), (cB,))
    P.dma("sp", msk, msk_d, (), (cB,))
    P.memset("dve", ones, 1.0, (cB,))
    ones32 = A.f32(128)
    P.memset("dve", ones32, 1.0, (cB,))
    P.memset("dve", epst, EPS, (cB,))
    P.memset("dve", one_t, 1.0, (cB,))
    P.act(clam, par[:, PC["lam"]:PC["lam"] + 16], AF.Exp, (cB,), (cB,), scale=-1.0)
    P.act(clam, clam, AF.Ln, (cB,), (cB,), bias=one_t[:, 0:1])
    P.ts("dve", clam, clam, -8.0, None, ALU.mult, None, (cB,), (cB,))
    P.ts("dve", gqs, par[:, PC["g_qn"]:PC["g_qn"] + 3], float(192.0 ** -0.5), None, ALU.mult, None, (cB,), (cB,))

    def pcol(name, i=0, n=1, parts=128):
        return par[0:parts, PC[name] + i: PC[name] + i + n]

    def slab_view(i, ncol=512, nrow=16):
        return wslab[i].rearrange("p (r c) -> p r c", c=512)[:, 0:nrow, 0:ncol]

    if "0" in stages:
        win_v = w_in.rearrange("(dc p) c -> p dc c", p=128)
        pesw_v = w_pesw.rearrange("(dc p) c -> p dc c", p=128)
        sv = wslab[0].rearrange("p (r c) -> p r c", c=512)
        P.dma("pq", sv[:, :, 0:320], win_v[:, :, 512:832], (), (slabB[0],))
        P.dma("pq", sv[:, :, 320:384], pesw_v, (), (slabB[0],))
        P.dma("pq", slab_view(1), win_v[:, :, 832:1344], (), (slabB[1],))
        P.dma("pq", slab_view(2), win_v[:, :, 1344:1856], (), (slabB[2],))
        P.dma("pq", slab_view(3), win_v[:, :, 0:512], (), (slabB[3],))
        P.dma("pq", slab_view(4), win_v[:, :, 1856:2368], (), (slabB[4],))
        P.dma("pq", slab_view(5), win_v[:, :, 2368:2880], (), (slabB[5],))

    s0b_done = []

    def emit_s0b():
        if s0b_done or "0" not in stages:
            return
        s0b_done.append(1)
        wo_v = w_out.rearrange("(g p) c -> p g c", p=128)
        for cg in range(4):
            P.dma("pq", slab_view(SL_OUT + cg), wo_v[:, :, cg * 512:(cg + 1) * 512], (), (slabB[SL_OUT + cg],))
        wg_v = w_gate.rearrange("(dc p) f -> p dc f", p=128)
        wu_v = w_up.rearrange("(dc p) f -> p dc f", p=128)
        for s in range(11):
            P.dma("pq", slab_view(SL_GU + 2 * s), wg_v[:, :, s * 512:(s + 1) * 512], (), (slabB[SL_GU + 2 * s],))
            P.dma("pq", slab_view(SL_GU + 2 * s + 1), wu_v[:, :, s * 512:(s + 1) * 512], (), (slabB[SL_GU + 2 * s + 1],))
        wd_v = w_down.rearrange("(f p) c -> p f c", p=128)
        for cg in range(4):
            for fg in range(4):
                i = SL_DN + cg * 4 + fg
                P.dma("pq", slab_view(i, 512, 11), wd_v[:, fg * 11:(fg + 1) * 11, cg * 512:(cg + 1) * 512], (), (slabB[i],))

    mark0 = A.top

    def rstd_from_bank(bk, bkB, scale, out_tile, outB, parts=128):
        P.act(out_tile[0:parts], bk[0:parts], AF.Ln, (bkB, cB), (outB,), bias=epst[0:parts, 0:1], scale=scale)
        P.act(out_tile[0:parts], out_tile[0:parts], AF.Exp, (outB,), (outB,), scale=-0.5)

    class SlabStream:
        def __init__(self, ring, seq):
            self.ring = ring
            self.seq = list(seq)
            self.issued = 0
            self.tiles = {}

        def ensure(self, upto):
            while self.issued < min(upto + 1, len(self.seq)):
                i = self.seq[self.issued]
                t, b = self.ring.get()
                P.dma("sp", t, wslab[i], (slabB[i],), (b,))
                self.tiles[self.issued] = (t.rearrange("p (r c) -> p r c", c=512), b)
                self.issued += 1

        def get(self, n, ahead=1):
            self.ensure(n + ahead)
            return self.tiles.pop(n)

    def prep_hT(src_ap, srcB, gname, hT, hTB_t, t, xn, xnB, ssq, ssqB, rr):
        P.act(xn, src_ap, AF.Square, (srcB,), (xnB, ssqB), accum=ssq[:, 0:1])
        P.act(ssq[:, 1:2], ssq[:, 0:1], AF.Ln, (ssqB, cB), (ssqB,), bias=epst[:, 0:1], scale=1.0 / D)
        P.act(ssq[:, 2:3], ssq[:, 1:2], AF.Exp, (ssqB,), (ssqB,), scale=-0.5)
        P.ts("dve", xn, src_ap, ssq[:, 2:3], None, ALU.mult, None, (srcB, ssqB), (xnB,))
        tpB = bankb[0]
        for dc in range(16):
            P.tr(PSB[:, dc * 128:(dc + 1) * 128], xn[:, dc * 128:(dc + 1) * 128], ident, (xnB, cB),
                 (bankb[0], bankb[1]), inc=(dc == 15))
        g3 = par[:, PC[gname]:PC[gname] + 16].unsqueeze(2).to_broadcast([128, 16, 128])
        P.tt("dve", hT[:, :, t * 128:(t + 1) * 128], PSB.rearrange("p (a b) -> p a b", b=128), g3, ALU.mult,
             (bankb[0], bankb[1], cB), (hTB_t,))

    for (jn, S, NOWN) in jobs:
        j = J[jn]
        NCK = S // CH
        NOC = NOWN // CH
        NSEG = S // NOWN
        SEGW = NOWN // CH
        mcol0 = 0 if NSEG == 4 else 4
        x_d, rope_d = j["x"], j["rope"]

        UBk = [[Buf(f"U{k_}_{b_}") for b_ in range(8)] for k_ in range(S // CH)]
        UhL = [Buf(f"UhL{b_}") for b_ in range(8)]
        UhR = [Buf(f"UhR{b_}") for b_ in range(8)]
        LR = {}

        def lru_setup(n_u, n_tp, banks):
            if "wrg" not in LR:
                LR["wrg"] = wrg_c
                LR["wB2"] = wB2_c
                LR["carry"] = carry_c
                LR["carB"] = [[Buf(f"carry{d_}_{b_}") for b_ in range(8)] for d_ in range(2)]
                P.memset("dve", carry_c, 0.0, [bb_ for l_ in LR["carB"] for bb_ in l_])
            LR["ub"] = Pool([A.f32(CH + 3) for _ in range(n_u)], track=True)
            LR["cpp"] = Pool([A.f32(CH) for _ in range(n_u)], track=True)
            LR["xcbp"] = Pool([A.bf16(CH) for _ in range(n_u)], track=True)
            LR["tp"] = Pool([A.f32(CH) for _ in range(n_tp)], track=True)
            LR["lpp"] = Pool([bank[b] for b in banks])
            LR["lpp"].bufs = [bankb[b] for b in banks]

        NCK = S // CH
        NOC = NOWN // CH
        NSEG = S // NOWN
        SEGW = NOWN // CH
        mcol0 = 0 if NSEG == 4 else 4
        HfB = [Buf(f"Hf{w}") for w in range(NOC)]
        MIXB = j["MIXB"] = [[Buf(f"MIX{g}_{w}") for w in range(NOC)] for g in range(16)]
        U = j["U"]

        def lru_unit_outer(w, d_, blks):
            ub, cpp, xcbp, tp, lpp = LR["ub"], LR["cpp"], LR["xcbp"], LR["tp"], LR["lpp"]
            wrg, wB2, carry, carB = LR["wrg"], LR["wB2"], LR["carry"], LR["carB"]
            def mcol(jb):
                return msk[:, mcol0 + jb - 1: mcol0 + jb]

            def lru_unit(w, d_, blks):
                own = w < NOC
                us, xcs, xcbs, gates, rs, is_, as_ = {}, {}, {}, {}, {}, {}, {}
                held = []
                for blk in blks:
                    u, uB = ub.get(); held.append((ub, uB))
                    deps_ = [UBk[w][blk], UBk[w - 1][blk] if w > 0 else UhL[blk],
                             UBk[w + 1][blk] if w + 1 < NCK else UhR[blk]]
                    P.dma("sp", u, U[blk, :, CH * w: CH * w + CH + 3], deps_, (uB,))
                    if (w % SEGW) == 0:
                        jb = w // SEGW if w > 0 else NSEG
                        P.ts("dve", u[:, 0:1], u[:, 0:1], mcol(jb), None, ALU.mult, None, (uB, cB), (uB,))
                    if ((w + 1) % SEGW) == 0:
                        jb = (w + 1) // SEGW
                        P.ts("dve", u[:, CH + 1:CH + 3], u[:, CH + 1:CH + 3], mcol(jb), None, ALU.mult, None,
                             (uB, cB), (uB,))
                    us[blk] = (u, uB)
                yield
                for blk in blks:
                    u, uB = us[blk]
                    cw = PC["conv_w"] + blk * 4
                    xc, xcB = cpp.get(); held.append((cpp, xcB))
                    P.act(xc, u[:, 0:CH], AF.Identity, (uB, cB), (xcB,), bias=pcol("conv_b", blk),
                          scale=par[:, cw:cw + 1])
                    xcs[blk] = (xc, xcB)
                yield
                for blk in blks:
                    u, uB = us[blk]
                    cw = PC["conv_w"] + blk * 4
                    xc, xcB = xcs[blk]
                    for tap in range(1, 4):
                        P.stt("dve", xc, u[:, tap:tap + CH], par[:, cw + tap:cw + tap + 1], xc, ALU.mult, ALU.add,
                              (uB, cB, xcB), (xcB,))
                yield
                for blk in blks:
                    xc, xcB = xcs[blk]
                    xcb, xcbB = xcbp.get(); held.append((xcbp, xcbB))
                    P.copy("act", xcb, xc, (xcB,), (xcbB,))
                    xcbs[blk] = (xcb, xcbB)
                yield
                for blk in blks:
                    xcb, xcbB = xcbs[blk]
                    rhs = xcb if d_ == 0 else xcb[:, ::-1]
                    br, brB = lpp.get()
                    P.mm(br, wrg[:, 0, d_, blk, :], rhs, True, True, (wB2, xcbB), (brB,))
                    bi, biB = lpp.get()
                    P.mm(bi, wrg[:, 1, d_, blk, :], rhs, True, True, (wB2, xcbB), (biB,))
                    r_, rB = tp.get(); held.append((tp, rB))
                    P.act(r_, br, AF.Sigmoid, (brB, cB), (rB,), bias=pcol("b_r", d_ * 8 + blk))
                    rs[blk] = (r_, rB)
                    i_, iB = tp.get(); held.append((tp, iB))
                    P.act(i_, bi, AF.Sigmoid, (biB, cB), (iB,), bias=pcol("b_i", d_ * 8 + blk))
                    is_[blk] = (i_, iB)
                yield
                for blk in blks:
                    r_, rB = rs[blk]
                    a_, aB = tp.get(); held.append((tp, aB))
                    P.act(a_, r_, AF.Exp, (rB, cB), (aB,), scale=clam[:, d_ * 8 + blk: d_ * 8 + blk + 1])
                    as_[blk] = (a_, aB)
                yield
                for blk in blks:
                    r_, rB = rs[blk]
                    a_, aB = as_[blk]
                    P.tt("pool", r_, a_, a_, ALU.mult, (aB,), (rB,))
                yield
                for blk in blks:
                    r_, rB = rs[blk]
                    P.act(r_, r_, AF.Sqrt, (rB, cB), (rB,), bias=one_t[:, 0:1], scale=-1.0)
                yield
                for blk in blks:
                    xc, xcB = xcs[blk]
                    xcv = xc if d_ == 0 else xc[:, ::-1]
                    r_, rB = rs[blk]
                    i_, iB = is_[blk]
                    P.tt("pool", i_, i_, xcv, ALU.mult, (iB, xcB), (iB,))
                    P.tt("pool", i_, i_, r_, ALU.mult, (iB, rB), (iB,))
                yield
                hs = {}
                for blk in blks:
                    i_, iB = is_[blk]
                    a_, aB = as_[blk]
                    cc = carry[:, d_ * 8 + blk: d_ * 8 + blk + 1]
                    ccB = carB[d_][blk]
                    if d_ == 0 and (w % SEGW) == 0:
                        jb = w // SEGW if w > 0 else NSEG
                        P.ts("dve", cc, cc, mcol(jb), None, ALU.mult, None, (ccB, cB), (ccB,))
                    if d_ == 1 and ((w + 1) % SEGW) == 0:
                        jb = (w + 1) // SEGW
                        P.ts("dve", cc, cc, mcol(jb), None, ALU.mult, None, (ccB, cB), (ccB,))
                    h_, hB = rs[blk]
                    P.scan(h_, a_, i_, cc, (aB, iB, ccB), (hB,))
                    P.copy("dve", cc, h_[:, CH - 1:CH], (hB,), (ccB,))
                    hs[blk] = (h_, hB)
                yield
                for blk in blks:
                    h_, hB = hs[blk]
                    if own and d_ == 0:
                        P.dma("sp", j["Hf"][blk, :, CH * w: CH * w + CH], h_, (hB,), (HfB[w],))
                    if own and d_ == 1:
                        hf, hfB = tp.get(); held.append((tp, hfB))
                        P.dma("sp", hf, j["Hf"][blk, :, CH * w: CH * w + CH], (HfB[w],), (hfB,))
                        gt, gtB = tp.get(); held.append((tp, gtB))
                        P.dma("sp", gt, j["G"][blk, :, CH * w: CH * w + CH], (j["GB"][w],) if "GB" in j else (), (gtB,))
                        P.tt("pool", hf, hf, h_[:, ::-1], ALU.add, (hfB, hB), (hfB,))
                        P.tt("pool", hf, hf, gt, ALU.mult, (hfB, gtB), (hfB,))
                        P.dma("sp", j["MIX"][8 + blk, :, CH * w: CH * w + CH], hf, (hfB,), (MIXB[8 + blk][w],))
                for pl_, b_ in held:
                    pl_.rel(b_)


            yield from lru_unit(w, d_, blks)

        def lockstep(gens, skew=0):
            active = list(enumerate(gens))
            t_ = 0
            while active:
                for item in list(active):
                    j_, g_ = item
                    if t_ < skew * j_:
                        continue
                    try:
                        next(g_)
                    except StopIteration:
                        active.remove(item)
                t_ += 1

        if "1" in stages:
            A.top = mark0
            wuq = A.bf16(4 * 1536).rearrange("p (c n) -> p c n", c=4)
            wuqs = A.bf16(4 * 512).rearrange("p (c n) -> p c n", c=4)
            wk = A.bf16(2 * 1024).rearrange("p (c h d) -> p c h d", c=2, h=8)
            wv = A.bf16(2 * 1024).rearrange("p (c h d) -> p c h d", c=2, h=8)
            wB = Buf("wres")
            P.dma("pq", wuq, w_uq.rearrange("(c p) n -> p c n", p=128), (), (wB,))
            P.dma("pq", wuqs, w_uqsw.rearrange("(c p) n -> p c n", p=128), (), (wB,))
            ukv_v = w_ukv.rearrange("(c p) (h two d) -> p c h two d", p=128, two=2, d=128)
            for c in range(2):
                P.dma("pq", wk[:, c], ukv_v[:, c, :, 0, :], (), (wB,))
                P.dma("pq", wv[:, c], ukv_v[:, c, :, 1, :], (), (wB,))
            ring = Pool([A.bf16(8192) for _ in range(2)])
            xt = Pool([A.f32(D) for _ in range(3)])
            xn4 = A.bf16(4 * D).rearrange("p (t d) -> p t d", t=4)
            xn4B = [Buf(f"xn{t}") for t in range(4)]
            ssqp = Pool([A.f32(4) for _ in range(4)])
            hT2 = [A.bf16(16 * CH).rearrange("p (a b) -> p a b", a=16) for _ in range(2)]
            hTB2 = [[Buf(f"hT{p_}_{t}") for t in range(4)] for p_ in range(2)]
            cf = A.f32(4 * CH).rearrange("p (a b) -> p a b", a=4)
            cfB = Buf("cf")
            cqn2 = [A.bf16(4 * CH).rearrange("p (a b) -> p a b", a=4) for _ in range(2)]
            cqnB2 = [Buf("cqn0"), Buf("cqn1")]
            ckvn2 = [A.bf16(2 * CH).rearrange("p (a b) -> p a b", a=2) for _ in range(2)]
            ckvnB2 = [Buf("ckvn0"), Buf("ckvn1")]
            kpf = A.f32(CH)
            kpsf = A.f32(CH)
            kptB = Buf("kptmp")
            kpr2 = [A.f32(CH) for _ in range(2)]
            sqkp2 = [A.bf16(CH) for _ in range(2)]
            kpB2 = [Buf("kp0"), Buf("kp1")]
            cst2 = [A.f32(2 * CH).rearrange("p (a b) -> p a b", a=2) for _ in range(2)]
            cstB2 = [Buf("cs0"), Buf("cs1")]
            vbuf = A.bf16(8 * 4 * 128).rearrange("p (h t d) -> p h t d", h=8, t=4)
            vB = Buf("vbuf")
            sqp_m = Pool([A.bf16(CH) for _ in range(4)])
            f32p_m = Pool([A.f32(CH) for _ in range(3)])
            sqp_h = Pool([A.bf16(CH) for _ in range(3)])
            f32p_h = Pool([A.f32(CH) for _ in range(3)])
            b16p_h = Pool([A.bf16(CH) for _ in range(4)])
            mp = Pool([bank[2], bank[3], bank[0], bank[1]])
            mp.bufs = [bankb[2], bankb[3], bankb[0], bankb[1]]
            hp = Pool([bank[b] for b in range(4, 8)])
            hp.bufs = [bankb[b] for b in range(4, 8)]

            order = list(range(NCK))
            seq = []
            for k in order:
                seq += [0, 1, 2] + ([3, 4, 5] if k < NOC else [])
            SS = SlabStream(ring, seq)
            st_ = {"sidx": 0}
            side = []

            def tick(n=1):
                for _ in range(n):
                    for g_ in list(side):
                        try:
                            next(g_)
                        except StopIteration:
                            side.remove(g_)

            def next_slab():
                r_ = SS.get(st_["sidx"])
                st_["sidx"] += 1
                return r_

            def proj16(slab, col0, m, hT, hTB, slB):
                tick(3)
                bk, bb = mp.get()
                for dc in range(16):
                    P.mm(bk[0:m], slab[:, dc, col0:col0 + m], hT[:, dc, :], dc == 0, dc == 15, (slB, *hTB), (bb,))
                return bk, bb

            xt_cur = {}

            def xprep_load(k, ts=(0, 1, 2)):
                c0 = k * CH
                for t in ts:
                    xt_t, xtB = xt.get()
                    P.dma("pq", xt_t, x_d[c0 + t * 128: c0 + (t + 1) * 128, :], (), (xtB,))
                    xt_cur[(k, t)] = (xt_t, xtB)

            def xprep_elem(k):
                for t in range(4):
                    if t == 1:
                        xprep_load(k, (3,))
                    xt_t, xtB = xt_cur.pop((k, t))
                    ssq, ssqB = ssqp.get()
                    xn = xn4[:, t, :]
                    P.act(xn, xt_t, AF.Square, (xtB,), (xn4B[t], ssqB), accum=ssq[:, 0:1])
                    P.act(ssq[:, 1:2], ssq[:, 0:1], AF.Ln, (ssqB, cB), (ssqB,), bias=epst[:, 0:1], scale=1.0 / D)
                    P.act(ssq[:, 2:3], ssq[:, 1:2], AF.Exp, (ssqB,), (ssqB,), scale=-0.5)
                    P.ts("dve", xn, xt_t, ssq[:, 2:3], None, ALU.mult, None, (xtB, ssqB), (xn4B[t],))

            def xprep_tr(par_):
                hT, hTB = hT2[par_], hTB2[par_]
                g3 = par[:, PC["g_attn"]:PC["g_attn"] + 16].unsqueeze(2).to_broadcast([128, 16, 128])
                for t in range(4):
                    for dc in range(16):
                        P.tr(PSB[:, dc * 128:(dc + 1) * 128], xn4[:, t, dc * 128:(dc + 1) * 128], ident, (xn4B[t], cB),
                             (bankb[0], bankb[1]), inc=(dc == 15))
                    P.tt("dve", hT[:, :, t * 128:(t + 1) * 128], PSB.rearrange("p (a b) -> p a b", b=128), g3, ALU.mult,
                         (bankb[0], bankb[1], cB), (hTB[t],))

            def nf_part1(ngrp, slab, slB, hT, hTB):
                sqs = []
                for g in range(ngrp):
                    bk, bb = proj16(slab, g * 128, 128, hT, hTB, slB)
                    P.copy("act", cf[:, g, :], bk, (bb,), (cfB,))
                    sq, sqB = sqp_m.get()
                    P.act(sq, bk, AF.Square, (bb,), (sqB,))
                    sqs.append((sq, sqB))
                return sqs

            def nf_part2(sqs, gname, width, outn, outB):
                ngrp = len(sqs)
                sb, sbB = mp.get()
                for g, (sq, sqB) in enumerate(sqs):
                    P.mm(sb, ones, sq, g == 0, g == ngrp - 1, (cB, sqB), (sbB,))
                rt, rtB = f32p_m.get()
                rstd_from_bank(sb, sbB, 1.0 / width, rt, rtB)
                for g in range(ngrp):
                    P.stt("dve", outn[:, g, :], cf[:, g, :], pcol(gname, g), rt, ALU.mult, ALU.mult,
                          (cfB, rtB, cB), (outB,))

            def qk_head(wn_fn, wp_fn, wps_fn, nc_, src, srcB, shared, cst, cstB, gn, gp, gps, dst_n, dst_p, c0):
                bn, bnB = hp.get()
                for c in range(nc_):
                    P.mm(bn, wn_fn(c), src[:, c, :], c == 0, c == nc_ - 1, (wB, srcB), (bnB,))
                sq, sqB = sqp_h.get()
                P.act(sq, bn, AF.Square, (bnB,), (sqB,))
                if shared is None:
                    bp, bpB = hp.get()
                    for c in range(nc_):
                        P.mm(bp[0:64], wp_fn(c), src[:, c, :], c == 0, c == nc_ - 1, (wB, srcB), (bpB,))
                    bs, bsB = hp.get()
                    for c in range(nc_):
                        P.mm(bs[0:64], wps_fn(c), src[:, c, :], c == 0, c == nc_ - 1, (wB, srcB), (bsB,))
                    sq2, sq2B = sqp_h.get()
                    P.act(sq2[0:64], bp[0:64], AF.Square, (bpB,), (sq2B,))
                else:
                    kpr, sq2, sq2B = shared
                yield
                sb, sbB = hp.get()
                P.mm(sb, ones, sq, True, False, (cB, sqB), (sbB,))
                P.mm(sb, ones[0:64, :], sq2[0:64], False, True, (cB, sq2B), (sbB,))
                rt, rtB = f32p_h.get()
                rstd_from_bank(sb, sbB, 1.0 / 192.0, rt, rtB)
                yield
                on, onB = b16p_h.get()
                P.stt("dve", on, bn, gn, rt, ALU.mult, ALU.mult, (bnB, rtB, cB), (onB,))
                P.dma("sp", dst_n[:, c0:c0 + CH], on, (onB,), ())
                op_, opB = b16p_h.get()
                if shared is None:
                    t1, t1B = f32p_h.get()
                    t2, t2B = f32p_h.get()
                    P.stt("dve", t1[0:64], bp[0:64], gp, cst[0:64, 0, :], ALU.mult, ALU.mult, (bpB, cstB, cB), (t1B,))
                    P.stt("dve", t2[0:64], bs[0:64], gps, cst[0:64, 1, :], ALU.mult, ALU.mult, (bsB, cstB, cB), (t2B,))
                    P.tt("dve", t1[0:64], t1[0:64], t2[0:64], ALU.add, (t1B, t2B), (t1B,))
                    P.tt("dve", op_[0:64], t1[0:64], rt[0:64], ALU.mult, (t1B, rtB), (opB,))
                else:
                    P.tt("dve", op_[0:64], kpr[0:64], rt[0:64], ALU.mult, (sq2B, rtB), (opB,))
                P.dma("sp", dst_p[:, c0:c0 + CH], op_[0:64], (opB,), ())
                yield

            def heads(k, par_):
                own = k < NOC
                c0 = k * CH
                ckvn, ckvnB = ckvn2[par_], ckvnB2[par_]
                cst, cstB = cst2[par_], cstB2[par_]
                for h in range(NH):
                    yield from qk_head(lambda c, h=h: wk[:, c, h, :], None, None, 2, ckvn, ckvnB,
                                       (kpr2[par_], sqkp2[par_], kpB2[par_]), cst, cstB,
                                       pcol("g_kn"), None, None, j["Kn"][h], j["Kp"][h], c0)
                for t in range(4):
                    for jj in range(2):
                        bk, bb = hp.get()
                        for c in range(2):
                            P.mm(bk, ckvn[:, c, t * 128:(t + 1) * 128], wv[:, c, 4 * jj:4 * jj + 4, :],
                                 c == 0, c == 1, (ckvnB, wB), (bb,))
                        P.copy("act" if jj == 0 else "dve", vbuf[:, 4 * jj:4 * jj + 4, t, :],
                               bk.rearrange("p (h d) -> p h d", h=4), (bb,), (vB,))
                    yield
                for h in range(NH):
                    P.dma("sp", j["V"][h, :, 4 * k:4 * k + 4, :], vbuf[:, h], (vB,), ())
                yield
                if own:
                    cqn, cqnB = cqn2[par_], cqnB2[par_]
                    for h in range(NH):
                        yield from qk_head(lambda c, h=h: wuq[:, c, h * 192:h * 192 + 128],
                                           lambda c, h=h: wuq[:, c, h * 192 + 128:h * 192 + 192],
                                           lambda c, h=h: wuqs[:, c, h * 64:(h + 1) * 64], 4, cqn, cqnB, None, cst, cstB,
                                           gqs[:, 0:1], gqs[0:64, 1:2], gqs[0:64, 2:3], j["Qn"][h], j["Qp"][h], c0)

            GB = j["GB"] = [Buf(f"G{k}") for k in range(NOC)]

            xprep_load(order[0])
            xprep_elem(order[0])
            xprep_tr(0)
            if len(order) > 1:
                xprep_load(order[1])
            emit_s0b()
            for i, k in enumerate(order):
                par_ = i % 2
                own = k < NOC
                c0 = k * CH
                hT, hTB = hT2[par_], hTB2[par_]
                cst, cstB = cst2[par_], cstB2[par_]
                kpr, sqkp, kpB = kpr2[par_], sqkp2[par_], kpB2[par_]
                if i + 1 < len(order) and i > 0:
                    xprep_load(order[i + 1])
                P.dma("sp", cst[0:64], rope_d[:, :, c0:c0 + CH].rearrange("a p n -> p a n"), (), (cstB,))
                sl, slB = next_slab()
                sq_ckv = nf_part1(2, sl, slB, hT, hTB)
                bk, bb = proj16(sl, 256, 64, hT, hTB, slB)
                P.copy("act", kpf[0:64], bk[0:64], (bb,), (kptB,))
                P.act(sqkp[0:64], bk[0:64], AF.Square, (bb,), (kpB,))
                bk, bb = proj16(sl, 320, 64, hT, hTB, slB)
                P.copy("act", kpsf[0:64], bk[0:64], (bb,), (kptB,))
                P.stt("dve", kpr[0:64], kpf[0:64], pcol("g_kp", 0, 1, 64), cst[0:64, 0, :], ALU.mult, ALU.mult,
                      (kptB, cstB, cB), (kpB,))
                P.stt("dve", kpsf[0:64], kpsf[0:64], pcol("g_kps", 0, 1, 64), cst[0:64, 1, :], ALU.mult, ALU.mult,
                      (kptB, cstB, cB), (kptB,))
                P.tt("dve", kpr[0:64], kpr[0:64], kpsf[0:64], ALU.add, (kpB, kptB), (kpB,))
                for s_ in range(2):
                    sl, slB = next_slab()
                    for g in range(4):
                        bk, bb = proj16(sl, g * 128, 128, hT, hTB, slB)
                        ft, ftB = f32p_m.get()
                        P.copy("act" if g % 2 == 0 else "dve", ft, bk, (bb,), (ftB,))
                        blk = s_ * 4 + g
                        P.dma("sp", j["U"][blk, :, 1 + c0: 1 + c0 + CH], ft, (ftB,), (UBk[k][blk],))
                        if k == 0:
                            P.dma("sp", j["U"][blk, :, 1 + S: 1 + S + 2], ft[:, 0:2], (ftB,), (UhR[blk],))
                        if k == NCK - 1:
                            P.dma("sp", j["U"][blk, :, 0:1], ft[:, CH - 1:CH], (ftB,), (UhL[blk],), slow=True)
                    if s_ == 0:
                        nf_part2(sq_ckv, "g_kva", 256.0, ckvn2[par_], ckvnB2[par_])
                        if i + 1 < len(order):
                            xprep_elem(order[i + 1])
                if own:
                    sl, slB = next_slab()
                    sq_cq = nf_part1(4, sl, slB, hT, hTB)
                    for s_ in range(2):
                        sl, slB = next_slab()
                        for g in range(4):
                            bk, bb = proj16(sl, g * 128, 128, hT, hTB, slB)
                            ft, ftB = f32p_m.get()
                            g2, g2B = f32p_m.get()
                            P.act(g2, bk, AF.Square, (bb,), (g2B,))
                            P.ts("dve", g2, g2, 0.044715, 1.0, ALU.mult, ALU.add, (g2B,), (g2B,))
                            P.tt("dve", g2, g2, bk, ALU.mult, (g2B, bb), (g2B,))
                            P.act(g2, g2, AF.Sigmoid, (g2B,), (g2B,), scale=float(2.0 * np.sqrt(2.0 / np.pi)))
                            P.tt("dve", ft, g2, bk, ALU.mult, (g2B, bb), (ftB,))
                            P.dma("sp", j["G"][s_ * 4 + g, :, c0:c0 + CH], ft, (ftB,), (GB[k],))
                        if s_ == 0:
                            nf_part2(sq_cq, "g_qa", 512.0, cqn2[par_], cqnB2[par_])
                while side:
                    tick()
                if i + 1 < len(order):
                    xprep_tr(1 - par_)
                side.append(heads(k, par_))
            while side:
                tick()
            P.barrier()

        if "2" in stages:
            A.top = mark0
            lru_setup(16, 48, list(range(8)))

            def lru_stream(ws, d_, half):
                for w in ws:
                    yield from lru_unit_outer(w, d_, [4 * half + b_ for b_ in range(4)])

            wf_, wb_ = list(range(NOC, NCK)), list(range(NCK - 1, NOC - 1, -1))
            lockstep([lru_stream(wf_, 0, 0), lru_stream(wb_, 1, 0), lru_stream(wf_, 0, 1), lru_stream(wb_, 1, 1)], skew=3)
            for w in range(NOC):
                lockstep([lru_unit_outer(w, 0, [0, 1, 2, 3]), lru_unit_outer(w, 0, [4, 5, 6, 7])], skew=5)
            for w in range(NOC - 1, -1, -1):
                lockstep([lru_unit_outer(w, 1, [0, 1, 2, 3]), lru_unit_outer(w, 1, [4, 5, 6, 7])], skew=5)
            P.barrier()

        if "3" in stages:
            A.top = mark0
            QB = 1024
            NQB = NOWN // QB
            KG = 2048 if S >= 2048 else S
            NT = KG // 128
            NKG = S // KG
            knp = Pool([A.bf16(KG) for _ in range(3)])
            kpp = Pool([A.bf16(KG) for _ in range(3)])
            vp = Pool([A.bf16(KG).rearrange("p (t d) -> p t d", d=128) for _ in range(3)])
            qnp = Pool([A.bf16(QB) for _ in range(2)])
            qpp = Pool([A.bf16(QB) for _ in range(2)])
            ptp = Pool([A.bf16(QB) for _ in range(4)])
            outp = Pool([A.f32(CH) for _ in range(4)])
            sp_ = Pool([PS[:, (2 + 2 * p) * 512:(4 + 2 * p) * 512] for p in range(3)])
            if "MIXB" not in j:
                j["MIXB"] = [[Buf() for w in range(NOC)] for g in range(16)]
            dacc = Pool([(A.f32(QB), A.f32(QB)) for _ in range(2)])
            dacc.bufs = [(Buf(), Buf()) for _ in range(2)]
            ob = [bank[0], bank[1]]
            obB = [bankb[0], bankb[1]]
            LA = 2
            for h in range(NH):
                for qb in range(NQB):
                    qn, qnB = qnp.get()
                    qp, qpB = qpp.get()
                    P.dma("sp", qn, j["Qn"][h, :, qb * QB:(qb + 1) * QB], (), (qnB,))
                    P.dma("sp", qp[0:64], j["Qp"][h, :, qb * QB:(qb + 1) * QB], (), (qpB,))
                    (accd, accq), (accdB, accqB) = dacc.get()
                    groups = {}

                    def ensure_group(kg):
                        if kg >= NKG or kg in groups:
                            return
                        kn, knB = knp.get()
                        kp, kpB_ = kpp.get()
                        vv, vvB = vp.get()
                        P.dma("sp", kn, j["Kn"][h, :, kg * KG:(kg + 1) * KG], (), (knB,))
                        P.dma("sp", kp[0:64], j["Kp"][h, :, kg * KG:(kg + 1) * KG], (), (kpB_,))
                        P.dma("sp", vv, j["V"][h, :, kg * NT:(kg + 1) * NT, :], (), (vvB,))
                        groups[kg] = (kn, knB, kp, kpB_, vv, vvB)

                    def tile(i):
                        kg, t = divmod(i, NT)
                        ensure_group(kg)
                        if t == 0:
                            ensure_group(kg + 1)
                        kn, knB, kp, kpB_, vv, vvB = groups[kg]
                        return (kn[:, t * 128:(t + 1) * 128], knB, kp[0:64, t * 128:(t + 1) * 128], kpB_,
                                vv[:, t, :], vvB)

                    n = NKG * NT
                    stq = {}

                    def emit_qk(i):
                        knt, knB, kpt, kpB_, vt, vvB = tile(i)
                        st, stB = sp_.get()
                        for hf_ in range(2):
                            P.mm(st[:, hf_ * 512:(hf_ + 1) * 512], knt, qn[:, hf_ * 512:(hf_ + 1) * 512], True, False,
                                 (knB, qnB), (stB,), inc=False)
                        for hf_ in range(2):
                            P.mm(st[:, hf_ * 512:(hf_ + 1) * 512], kpt, qp[0:64, hf_ * 512:(hf_ + 1) * 512], False, True,
                                 (kpB_, qpB), (stB,), inc=(hf_ == 1))
                        pt, ptB = ptp.get()
                        P.act(pt, st, AF.Exp, (stB,), (ptB,))
                        stq[i] = (pt, ptB, vt, vvB)

                    for i in range(min(LA, n)):
                        emit_qk(i)
                    cnt = {"dve": 0, "pool": 0}
                    for i in range(n):
                        if i + LA < n:
                            emit_qk(i + LA)
                        pt, ptB, vt, vvB = stq.pop(i)
                        for hf_ in range(2):
                            P.mm(ob[hf_], vt, pt[:, hf_ * 512:(hf_ + 1) * 512], i == 0, i == n - 1, (vvB, ptB), (obB[hf_],),
                                 inc=(hf_ == 1))
                        en_ = "pool" if i % 3 == 2 else "dve"
                        ac_, acB_ = (accd, accdB) if en_ == "dve" else (accq, accqB)
                        if cnt[en_] == 0:
                            P.copy(en_, ac_, pt, (ptB,), (acB_,))
                        else:
                            P.tt(en_, ac_, ac_, pt, ALU.add, (acB_, ptB), (acB_,))
                        cnt[en_] += 1
                    if cnt["pool"]:
                        P.tt("dve", accd, accd, accq, ALU.add, (accdB, accqB), (accdB,))
                    db, dbB = sp_.get()
                    for hf_ in range(2):
                        P.mm(db[:, hf_ * 512:(hf_ + 1) * 512], ones32, accd[:, hf_ * 512:(hf_ + 1) * 512], True, True,
                             (cB, accdB), (dbB,), inc=(hf_ == 1))
                    for hf_ in range(2):
                        rd, rdB = outp.get()
                        P.recip(rd, db[:, hf_ * 512:(hf_ + 1) * 512], (dbB,), (rdB,))
                        P.tt("dve", rd, ob[hf_], rd, ALU.mult, (obB[hf_], rdB), (rdB,))
                        c0_ = qb * QB + hf_ * 512
                        P.dma("sp", j["MIX"][h, :, c0_:c0_ + 512], rd, (rdB,), (j["MIXB"][h][c0_ // CH],))
            P.barrier()

        if "4" in stages:
            emit_s0b()
            A.top = mark0
            ring = Pool([A.bf16(8192) for _ in range(3)])
            xres = [A.f32(D) for _ in range(4)]
            xrB = [Buf(f"xres{t}") for t in range(4)]
            xn = A.bf16(D)
            xnB = Buf("xn4")
            ssq = A.f32(4)
            ssqB = Buf("ssq4")
            hT = A.bf16(16 * CH).rearrange("p (a b) -> p a b", a=16)
            hTB = [Buf(f"h2T{t}") for t in range(4)]
            mixed2 = [A.bf16(16 * CH).rearrange("p (a b) -> p a b", a=16) for _ in range(2)]
            mixedB2 = [Buf("mixed0"), Buf("mixed1")]
            big = A.bf16(NFC * CH)
            actT = big.rearrange("p (a b) -> p a b", a=NFC)
            bigB = Buf("big")
            stg = hT.rearrange("p a b -> p (a b)").bitcast(F32).rearrange("p (a b) -> p a b", a=8)
            pp23 = Pool([bank[2], bank[3]])
            pp23.bufs = [bankb[2], bankb[3]]
            sqp = Pool([A.bf16(CH) for _ in range(8)])
            f32p = Pool([A.f32(CH) for _ in range(4)])
            pp = Pool([bank[b] for b in range(2, 8)])
            pp.bufs = [bankb[b] for b in range(2, 8)]
            seq = []
            for k in range(NOC):
                seq += [SL_OUT + c for c in range(4)] + [SL_GU + s for s in range(22)] + [SL_DN + s for s in range(16)]
            SS = SlabStream(ring, seq)
            sidx = 0
            MIXB = j.get("MIXB") or [[Buf() for w in range(NOC)] for g in range(16)]
            def mix_prep(k):
                c0 = k * CH
                mixed, mixedB = mixed2[k % 2], mixedB2[k % 2]
                for half in range(2):
                    for g in range(8):
                        gg = half * 8 + g
                        P.dma("sp", stg[:, g, :], j["MIX"][gg, :, c0:c0 + CH], (MIXB[gg][k],), hTB)
                    yield
                    sqs = []
                    for g in range(8):
                        sq, sqB = sqp.get()
                        P.act(sq, stg[:, g, :], AF.Square, hTB, (sqB,))
                        sqs.append((sq, sqB))
                    yield
                    sb, sbB = pp23.get()
                    for g, (sq, sqB) in enumerate(sqs):
                        P.mm(sb, ones, sq, g == 0, g == 7, (cB, sqB), (sbB,))
                    rt, rtB = f32p.get()
                    rstd_from_bank(sb, sbB, 1.0 / 1024.0, rt, rtB)
                    yield
                    gname = "g_ao" if half == 0 else "g_lo"
                    for g in range(8):
                        gg = half * 8 + g
                        P.stt("dve", mixed[:, gg, :], stg[:, g, :], pcol(gname, g), rt, ALU.mult, ALU.mult,
                              (*hTB, rtB, cB), (mixedB,))
                    yield

            for _ in mix_prep(0):
                pass
            for k in range(NOC):
                c0 = k * CH
                mixed, mixedB = mixed2[k % 2], mixedB2[k % 2]
                for t in range(4):
                    P.dma("pq", xres[t], x_d[c0 + t * 128:c0 + (t + 1) * 128, :], (), (xrB[t],))
                for cg in range(4):
                    sl, slB = SS.get(sidx); sidx += 1
                    for t in range(4):
                        bk, bb = pp.get()
                        for g in range(16):
                            P.mm(bk, mixed[:, g, t * 128:(t + 1) * 128], sl[:, g, :], g == 0, g == 15, (mixedB, slB), (bb,))
                        xs = xres[t][:, cg * 512:(cg + 1) * 512]
                        P.tt("dve", xs, bk, xs, ALU.add, (bb, xrB[t]), (xrB[t],))
                for t in range(4):
                    prep_hT(xres[t], xrB[t], "g_ffn", hT, hTB[t], t, xn, xnB, ssq, ssqB, None)
                for s in range(11):
                    sg, sgB = SS.get(sidx); sidx += 1
                    su, suB = SS.get(sidx); sidx += 1
                    for fc in range(4):
                        bg, bgB = pp.get()
                        for dc in range(16):
                            P.mm(bg, sg[:, dc, fc * 128:(fc + 1) * 128], hT[:, dc, :], dc == 0, dc == 15, (sgB, *hTB), (bgB,))
                        bu, buB = pp.get()
                        for dc in range(16):
                            P.mm(bu, su[:, dc, fc * 128:(fc + 1) * 128], hT[:, dc, :], dc == 0, dc == 15, (suB, *hTB), (buB,))
                        ft, ftB = f32p.get()
                        P.act(ft, bg, AF.Silu, (bgB,), (ftB,))
                        P.tt("dve", actT[:, s * 4 + fc, :], ft, bu, ALU.mult, (ftB, buB), (bigB,))
                accb = [bank[b] for b in range(4, 8)]
                accB = [bankb[b] for b in range(4, 8)]
                nxt = mix_prep(k + 1) if k + 1 < NOC else iter(())
                for cg in range(4):
                    for fg in range(4):
                        sl, slB = SS.get(sidx); sidx += 1
                        next(nxt, None)
                        for jf in range(11):
                            f = fg * 11 + jf
                            for t in range(4):
                                P.mm(accb[t], actT[:, f, t * 128:(t + 1) * 128], sl[:, jf, :], f == 0, f == NFC - 1,
                                     (bigB, slB), (accB[t],), inc=(f == NFC - 1) or (jf == 10 and t == 3))
                    for t in range(4):
                        xs = xres[t][:, cg * 512:(cg + 1) * 512]
                        P.tt("dve", xs, accb[t], xs, ALU.add, (accB[t], xrB[t]), (xrB[t],))
                for _ in nxt:
                    pass
                for t in range(4):
                    P.dma("sp", j["y"][c0 + t * 128:c0 + (t + 1) * 128, :], xres[t], (xrB[t],), ())
            P.barrier()

    P.barrier()
    with nc.Block() as block:
        @block.tensor
        def _(e):
            P.replay("pe", e)

        @block.scalar
        def _(e):
            P.replay("act", e)

        @block.vector
        def _(e):
            P.replay("dve", e)

        @block.gpsimd
        def _(e):
            P.replay("pool", e)

        @block.sync
        def _(e):
            P.replay("sp", e)
    return nc, P


def _pack_params(inp):
    p = np.zeros((128, NPAR), np.float32)

    def put(name, arr):
        p[:arr.shape[0], PC[name]:PC[name] + arr.shape[1]] = arr
    f = lambda a: np.asarray(a, np.float32)
    put("g_attn", f(inp["attn_norm_g"])[0].reshape(16, 128).T)
    put("g_ffn", f(inp["ffn_norm_g"])[0].reshape(16, 128).T)
    put("g_qa", f(inp["q_a_norm_g"])[0].reshape(4, 128).T)
    put("g_kva", f(inp["kv_a_norm_g"])[0].reshape(2, 128).T)
    sw = (np.arange(64) + 32) % 64
    gq = f(inp["q_norm_g"])[0]
    gk = f(inp["k_norm_g"])[0]
    put("g_qn", gq[:128, None])
    put("g_qp", gq[128:, None])
    put("g_qps", gq[128:][sw][:, None])
    put("g_kn", gk[:128, None])
    put("g_kp", gk[128:, None])
    put("g_kps", gk[128:][sw][:, None])
    cw = f(inp["conv_w"])[0]
    put("conv_w", cw.reshape(4, 8, 128).transpose(2, 1, 0).reshape(128, 32))
    put("conv_b", f(inp["conv_b"])[0].reshape(8, 128).T)
    put("b_r", f(inp["b_rg_r"])[0].reshape(16, 128).T)
    put("b_i", f(inp["b_rg_i"])[0].reshape(16, 128).T)
    put("lam", f(inp["lru_lambda"])[0].reshape(16, 128).T)
    put("g_ao", f(inp["attn_out_norm_g"])[0].reshape(8, 128).T)
    put("g_lo", f(inp["lru_out_norm_g"])[0].reshape(8, 128).T)
    return p


def _rope_table(pos):
    half = 32
    inv_freq = (np.float32(10000.0) ** (np.float32(-2.0) * np.arange(half, dtype=np.float32) / np.float32(64))).astype(np.float32)
    ang = (pos.astype(np.float32)[:, None] * inv_freq[None, :]).astype(np.float32)
    c = np.cos(ang.astype(np.float64)).astype(np.float32).T
    s = np.sin(ang.astype(np.float64)).astype(np.float32).T
    cosT = np.concatenate([c, c], 0)
    sinT = np.concatenate([-s, s], 0)
    return np.ascontiguousarray(np.stack([cosT, sinT], 0))


_CACHE = {}


def kernel(**inputs):
    f = lambda a: np.ascontiguousarray(np.asarray(a, np.float32))
    xp = f(inputs["x_prompt"])
    xs = f(inputs["x_sample"])
    cfg = {"jobs": [("p", 16384, 4096), ("s", 2048, 1024)], "debug": False}
    if "nc" not in _CACHE:
        _CACHE["nc"] = build(cfg)
    nc, P = _CACHE["nc"]
    params = _pack_params(inputs)
    sw = (np.arange(64) + 32) % 64
    w_in = f(inputs["w_in"])[0]
    w_pesw = np.ascontiguousarray(w_in[:, 768:832][:, sw])
    w_uq = f(inputs["w_uq"])[0]
    w_uqsw = np.ascontiguousarray(w_uq.reshape(512, 8, 192)[:, :, 128:][:, :, sw].reshape(512, 512))
    w_rg = np.ascontiguousarray(np.stack([f(inputs["w_rg_r"])[0], f(inputs["w_rg_i"])[0]], 0))
    common = {
        "w_in": w_in, "w_pesw": w_pesw, "w_uq": w_uq, "w_uqsw": w_uqsw, "w_ukv": f(inputs["w_ukv"])[0],
        "w_rg": w_rg, "w_out": f(inputs["w_out"])[0], "w_gate": f(inputs["w_gate"])[0], "w_up": f(inputs["w_up"])[0],
        "w_down": f(inputs["w_down"])[0], "ident": np.eye(128, dtype=np.float32), "params": params,
    }
    in_maps = []
    for c in range(8):
        pb, pq = c // 4, c % 4
        sb, sh = c // 2, c % 2
        m = dict(common)
        m["xp"] = np.ascontiguousarray(np.roll(xp[pb], -pq * 4096, axis=0))
        m["xs"] = np.ascontiguousarray(np.roll(xs[sb], -sh * 1024, axis=0))
        m["ropep"] = _rope_table((np.arange(16384) + pq * 4096) % 16384)
        m["ropes"] = _rope_table((np.arange(2048) + sh * 1024) % 2048)
        mk = np.ones((128, 8), np.float32)
        mk[:, (4 - pq) - 1] = 0.0
        mk[:, 4 + (2 - sh) - 1] = 0.0
        m["msk"] = mk
        in_maps.append(m)
    res = run_bass_kernel_spmd(nc, in_maps, core_ids=list(range(8)))
    yp = np.zeros((2, 16384, D), np.float32)
    ys = np.zeros((4, 2048, D), np.float32)
    for c in range(8):
        pb, pq = c // 4, c % 4
        sb, sh = c // 2, c % 2
        yp[pb, pq * 4096:(pq + 1) * 4096] = res.results[c]["yp"]
        ys[sb, sh * 1024:(sh + 1) * 1024] = res.results[c]["ys"]
    return (yp, ys)
```

```python
import os
import numpy as np
import concourse.bass as bass
import concourse.mybir as mybir
from concourse.bass_utils import run_bass_kernel_spmd

F32 = mybir.dt.float32
BF16 = mybir.dt.bfloat16
ALU = mybir.AluOpType
AF = mybir.ActivationFunctionType

D = 2048
NH = 8
DFF = 5632
NFC = DFF // 128
EPS = 1e-6
CH = 512
NQD = 16

PC = {}
_o = 0
for _n, _w in [("g_attn", 16), ("g_ffn", 16), ("g_qa", 4), ("g_kva", 2), ("g_qn", 1), ("g_qp", 1), ("g_qps", 1),
               ("g_kn", 1), ("g_kp", 1), ("g_kps", 1), ("conv_w", 32), ("conv_b", 8), ("b_r", 16), ("b_i", 16),
               ("lam", 16), ("g_ao", 8), ("g_lo", 8)]:
    PC[_n] = _o
    _o += _w
NPAR = _o


class Buf:
    __slots__ = ("name", "w", "r")

    def __init__(self, name=""):
        self.name = name
        self.w = None
        self.r = {}


class Eng:
    def __init__(self, name, is_dma=False):
        self.name = name
        self.is_dma = is_dma
        self.ops = []
        self.seen = {}
        self.count = 0
        self.sem = None
        self.qsems = []
        self.qcnt = []
        self.rr = 0
        self.pending = []


class Prog:
    def __init__(self, nc):
        self.nc = nc
        self.sems = []
        self.eng = {}
        for n in ("pe", "act", "dve", "pool"):
            e = Eng(n)
            e.sem = self._newsem(n)
            self.eng[n] = e
        for n in ("sp", "pq"):
            e = Eng(n, True)
            e.qsems = [self._newsem(f"{n}{i}") for i in range(NQD)]
            e.qcnt = [0] * NQD
            self.eng[n] = e
        self.n_ops = 0

    def _newsem(self, name):
        self.sems.append(self.nc.alloc_semaphore("s_" + name))
        return len(self.sems) - 1

    def _stream(self, en):
        return self.eng["pool"] if en == "pq" else self.eng[en]

    def _waits(self, en, reads, writes, extra=()):
        st = self._stream(en)
        need = {}

        def add(tok):
            if tok is None:
                return
            s, v = tok
            if need.get(s, 0) < v:
                need[s] = v
        for b in reads:
            add(b.w)
        for b in writes:
            add(b.w)
            for s, v in b.r.items():
                add((s, v))
        for t in extra:
            add(t)
        out = []
        for s, v in need.items():
            if en == "pe" and s == self.eng["pe"].sem:
                continue
            if st.seen.get(s, 0) >= v:
                continue
            st.seen[s] = v
            out.append((s, v))
        return out

    def _commit(self, tok, reads, writes):
        s, v = tok
        for b in reads:
            if b.r.get(s, 0) < v:
                b.r[s] = v
        for b in writes:
            b.w = tok
            b.r = {}

    def op(self, en, fn, R=(), W=(), inc=True):
        e = self.eng[en]
        waits = self._waits(en, R, W)
        self.n_ops += 1
        if inc:
            e.count += 1
            tok = (e.sem, e.count)
            e.ops.append((waits, fn, (e.sem, 1)))
            self._commit(tok, R, W)
            for (r, w) in e.pending:
                self._commit(tok, r, w)
            e.pending = []
        else:
            e.ops.append((waits, fn, None))
            e.pending.append((tuple(R), tuple(W)))

    def dma(self, q, out, in_, R=(), W=(), slow=False):
        e = self.eng[q]
        i = e.rr % NQD
        e.rr += 1
        extra = []
        if e.qcnt[i] > 0:
            extra.append((e.qsems[i], 16 * e.qcnt[i]))
        waits = self._waits(q, R, W, extra)
        e.qcnt[i] += 1
        tok = (e.qsems[i], 16 * e.qcnt[i])
        st = self._stream(q)
        if slow:
            st.ops.append((waits, (lambda eng, o=out, s=in_: eng.dma_start(out=o, in_=s, allow_slow_non_contiguous=True)),
                           (e.qsems[i], 16)))
        else:
            st.ops.append((waits, (lambda eng, o=out, s=in_: eng.dma_start(out=o, in_=s)), (e.qsems[i], 16)))
        self._commit(tok, R, W)
        self.n_ops += 1

    def barrier(self):
        toks = []
        for n in ("pe", "act", "dve", "pool"):
            e = self.eng[n]
            assert not e.pending
            if e.count:
                toks.append((e.sem, e.count))
        for n in ("sp", "pq"):
            e = self.eng[n]
            for i in range(NQD):
                if e.qcnt[i]:
                    toks.append((e.qsems[i], 16 * e.qcnt[i]))
        for n in ("pe", "act", "dve", "pool", "sp"):
            st = self.eng[n]
            ws = []
            for s, v in toks:
                if st.seen.get(s, 0) >= v:
                    continue
                if n != "sp" and s == st.sem:
                    pass
                st.seen[s] = v
                ws.append((s, v))
            if ws:
                st.ops.append((ws, None, None))

    def replay(self, en, e):
        for waits, fn, inc in self.eng[en].ops:
            for s, v in waits:
                e.wait_ge(self.sems[s], v)
            if fn is None:
                continue
            ins = fn(e)
            if inc is not None:
                ins.then_inc(self.sems[inc[0]], inc[1])

    def mm(self, out, lhsT, rhs, start, stop, R, W, inc=None):
        if inc is None:
            inc = stop
        self.op("pe", lambda e: e.matmul(out, lhsT=lhsT, rhs=rhs, start=start, stop=stop), R, W, inc)

    def tr(self, out, in_, ident, R, W, inc=True):
        self.op("pe", lambda e: e.transpose(out, in_, ident), R, W, inc)

    def act(self, out, in_, func, R, W, bias=None, scale=None, accum=None, en="act"):
        kw = {}
        if bias is not None:
            kw["bias"] = bias
        if scale is not None:
            kw["scale"] = scale
        if accum is not None:
            kw["accum_out"] = accum
        self.op(en, lambda e: e.activation(out=out, in_=in_, func=func, **kw), R, W)

    def tt(self, en, out, in0, in1, op, R, W):
        self.op(en, lambda e: e.tensor_tensor(out=out, in0=in0, in1=in1, op=op), R, W)

    def ts(self, en, out, in0, s1, s2, op0, op1, R, W):
        if s2 is None:
            self.op(en, lambda e: e.tensor_scalar(out=out, in0=in0, scalar1=s1, scalar2=None, op0=op0), R, W)
        else:
            self.op(en, lambda e: e.tensor_scalar(out=out, in0=in0, scalar1=s1, scalar2=s2, op0=op0, op1=op1), R, W)

    def stt(self, en, out, in0, scalar, in1, op0, op1, R, W):
        self.op(en, lambda e: e.scalar_tensor_tensor(out=out, in0=in0, scalar=scalar, in1=in1, op0=op0, op1=op1), R, W)

    def copy(self, en, out, in_, R, W):
        if en == "act":
            self.op(en, lambda e: e.activation(out=out, in_=in_, func=AF.Copy), R, W)
        else:
            self.op(en, lambda e: e.tensor_copy(out=out, in_=in_), R, W)

    def recip(self, out, in_, R, W):
        self.op("dve", lambda e: e.reciprocal(out=out, in_=in_), R, W)

    def scan(self, out, d0, d1, init, R, W):
        self.op("dve", lambda e: e.tensor_tensor_scan(out=out, data0=d0, data1=d1, initial=init,
                                                        op0=ALU.mult, op1=ALU.add), R, W)

    def memset(self, en, ap, val, W):
        self.op(en, lambda e: e.memset(ap, val), (), W)


class Arena:
    def __init__(self, nc, nbytes):
        self.t = nc.alloc_sbuf_tensor("arena", [128, nbytes // 4], F32).ap()
        self.top = 0
        self.cap = nbytes

    def f32(self, n, parts=128):
        off = self.top
        self.top += ((n * 4 + 31) // 32) * 32
        assert self.top <= self.cap, f"SBUF overflow {self.top}"
        return self.t[0:parts, off // 4: off // 4 + n]

    def bf16(self, n, parts=128):
        assert n % 2 == 0
        off = self.top
        self.top += ((n * 2 + 31) // 32) * 32
        assert self.top <= self.cap, f"SBUF overflow {self.top}"
        return self.t[0:parts, off // 4: off // 4 + n // 2].bitcast(BF16)


class Pool:
    def __init__(self, tiles, track=False):
        self.tiles = tiles
        self.bufs = [Buf() for _ in tiles]
        self.i = 0
        self.track = track
        self.held = [False] * len(tiles)

    def get(self):
        n = len(self.tiles)
        for _ in range(n):
            k = self.i % n
            self.i += 1
            if not self.held[k]:
                break
        else:
            raise RuntimeError("pool exhausted")
        if self.track:
            self.held[k] = True
        return self.tiles[k], self.bufs[k]

    def rel(self, buf):
        for k, b in enumerate(self.bufs):
            if b is buf:
                self.held[k] = False
                return
        raise KeyError("buf not in pool")


def build(cfg):
    nc = bass.Bass("TRN2", target_bir_lowering=False)
    P = Prog(nc)
    jobs = cfg["jobs"]
    dbg = cfg.get("debug", False)
    stages = cfg.get("stages", "01234")

    def din(name, shape, dt=F32):
        return nc.dram_tensor(name, list(shape), dt, kind="ExternalInput").ap()

    def dscr(name, shape, dt):
        kind = "ExternalOutput" if dbg else "Internal"
        return nc.dram_tensor(name, list(shape), dt, kind=kind).ap()

    w_in = din("w_in", [D, 2880])
    w_pesw = din("w_pesw", [D, 64])
    w_uq = din("w_uq", [512, 1536])
    w_uqsw = din("w_uqsw", [512, 512])
    w_ukv = din("w_ukv", [256, 2048])
    w_rg = din("w_rg", [2, 2, 8, 128, 128])
    w_out = din("w_out", [D, D])
    w_gate = din("w_gate", [D, DFF])
    w_up = din("w_up", [D, DFF])
    w_down = din("w_down", [DFF, D])
    ident_d = din("ident", [128, 128])
    params_d = din("params", [128, NPAR])
    msk_d = din("msk", [128, 8])

    J = {}
    for (jn, S, NOWN) in jobs:
        j = {"S": S, "NOWN": NOWN}
        j["x"] = din("x" + jn, [S, D])
        j["rope"] = din("rope" + jn, [2, 64, S])
        j["y"] = nc.dram_tensor("y" + jn, [NOWN, D], F32, kind="ExternalOutput").ap()
        j["Kn"] = dscr("Kn" + jn, [NH, 128, S], BF16)
        j["Kp"] = dscr("Kp" + jn, [NH, 64, S], BF16)
        j["V"] = dscr("V" + jn, [NH, 128, S // 128, 128], BF16)
        j["Qn"] = dscr("Qn" + jn, [NH, 128, NOWN], BF16)
        j["Qp"] = dscr("Qp" + jn, [NH, 64, NOWN], BF16)
        j["U"] = dscr("U" + jn, [8, 128, S + 8], F32)
        j["G"] = dscr("G" + jn, [8, 128, NOWN], F32)
        j["Hf"] = dscr("Hf" + jn, [8, 128, NOWN], F32)
        j["MIX"] = dscr("MIX" + jn, [16, 128, NOWN], F32)
        J[jn] = j
    NSL_IN = 6
    NSL = NSL_IN + 4 + 22 + 16
    wslab = nc.dram_tensor("wslab", [NSL, 128, 8192], BF16, kind="Internal").ap()
    SL_OUT = NSL_IN
    SL_GU = SL_OUT + 4
    SL_DN = SL_GU + 22

    A = Arena(nc, 212480)
    PS = nc.alloc_psum_tensor("ps", [128, 4096], F32).ap()
    bank = [PS[:, b * 512:(b + 1) * 512] for b in range(8)]
    bankb = [Buf(f"bank{b}") for b in range(8)]
    PSB = PS[:, 0:1024].bitcast(BF16)

    ident = A.bf16(128)
    ones = A.bf16(128)
    par = A.f32(NPAR)
    msk = A.f32(8)
    clam = A.f32(16)
    gqs = A.f32(3)
    epst = A.f32(1)
    one_t = A.f32(1)
    cB = Buf("consts")
    slabB = [Buf(f"slab{i}") for i in range(NSL)]
    wrg_c = A.bf16(2 * 2 * 8 * 128).rearrange("p (g d n o) -> p g d n o", g=2, d=2, n=8)
    wB2_c = Buf("wrg")
    carry_c = A.f32(16)
    for g_ in range(2):
        for d__ in range(2):
            P.dma("pq", wrg_c[:, g_, d__], w_rg[g_, d__].rearrange("n c o -> c n o"), (), (wB2_c,))

    P.dma("pq", ident, ident_d, (), (cB,))
    P.dma("sp", par, params_d, (), (cB,))
    P.dma("sp", msk, msk_d, (), (cB,))
    P.memset("dve", ones, 1.0, (cB,))
    ones32 = A.f32(128)
    P.memset("dve", ones32, 1.0, (cB,))
    P.memset("dve", epst, EPS, (cB,))
    P.memset("dve", one_t, 1.0, (cB,))
    P.act(clam, par[:, PC["lam"]:PC["lam"] + 16], AF.Exp, (cB,), (cB,), scale=-1.0)
    P.act(clam, clam, AF.Ln, (cB,), (cB,), bias=one_t[:, 0:1])
    P.ts("dve", clam, clam, -8.0, None, ALU.mult, None, (cB,), (cB,))
    P.ts("dve", gqs, par[:, PC["g_qn"]:PC["g_qn"] + 3], float(192.0 ** -0.5), None, ALU.mult, None, (cB,), (cB,))

    def pcol(name, i=0, n=1, parts=128):
        return par[0:parts, PC[name] + i: PC[name] + i + n]

    def slab_view(i, ncol=512, nrow=16):
        return wslab[i].rearrange("p (r c) -> p r c", c=512)[:, 0:nrow, 0:ncol]

    if "0" in stages:
        win_v = w_in.rearrange("(dc p) c -> p dc c", p=128)
        pesw_v = w_pesw.rearrange("(dc p) c -> p dc c", p=128)
        sv = wslab[0].rearrange("p (r c) -> p r c", c=512)
        P.dma("pq", sv[:, :, 0:320], win_v[:, :, 512:832], (), (slabB[0],))
        P.dma("pq", sv[:, :, 320:384], pesw_v, (), (slabB[0],))
        P.dma("pq", slab_view(1), win_v[:, :, 832:1344], (), (slabB[1],))
        P.dma("pq", slab_view(2), win_v[:, :, 1344:1856], (), (slabB[2],))
        P.dma("pq", slab_view(3), win_v[:, :, 0:512], (), (slabB[3],))
        P.dma("pq", slab_view(4), win_v[:, :, 1856:2368], (), (slabB[4],))
        P.dma("pq", slab_view(5), win_v[:, :, 2368:2880], (), (slabB[5],))

    s0b_done = []

    def emit_s0b():
        if s0b_done or "0" not in stages:
            return
        s0b_done.append(1)
        wo_v = w_out.rearrange("(g p) c -> p g c", p=128)
        for cg in range(4):
            P.dma("pq", slab_view(SL_OUT + cg), wo_v[:, :, cg * 512:(cg + 1) * 512], (), (slabB[SL_OUT + cg],))
        wg_v = w_gate.rearrange("(dc p) f -> p dc f", p=128)
        wu_v = w_up.rearrange("(dc p) f -> p dc f", p=128)
        for s in range(11):
            P.dma("pq", slab_view(SL_GU + 2 * s), wg_v[:, :, s * 512:(s + 1) * 512], (), (slabB[SL_GU + 2 * s],))
            P.dma("pq", slab_view(SL_GU + 2 * s + 1), wu_v[:, :, s * 512:(s + 1) * 512], (), (slabB[SL_GU + 2 * s + 1],))
        wd_v = w_down.rearrange("(f p) c -> p f c", p=128)
        for cg in range(4):
            for fg in range(4):
                i = SL_DN + cg * 4 + fg
                P.dma("pq", slab_view(i, 512, 11), wd_v[:, fg * 11:(fg + 1) * 11, cg * 512:(cg + 1) * 512], (), (slabB[i],))

    mark0 = A.top

    def rstd_from_bank(bk, bkB, scale, out_tile, outB, parts=128):
        P.act(out_tile[0:parts], bk[0:parts], AF.Ln, (bkB, cB), (outB,), bias=epst[0:parts, 0:1], scale=scale)
        P.act(out_tile[0:parts], out_tile[0:parts], AF.Exp, (outB,), (outB,), scale=-0.5)

    class SlabStream:
        def __init__(self, ring, seq):
            self.ring = ring
            self.seq = list(seq)
            self.issued = 0
            self.tiles = {}

        def ensure(self, upto):
            while self.issued < min(upto + 1, len(self.seq)):
                i = self.seq[self.issued]
                t, b = self.ring.get()
                P.dma("sp", t, wslab[i], (slabB[i],), (b,))
                self.tiles[self.issued] = (t.rearrange("p (r c) -> p r c", c=512), b)
                self.issued += 1

        def get(self, n, ahead=1):
            self.ensure(n + ahead)
            return self.tiles.pop(n)

    def prep_hT(src_ap, srcB, gname, hT, hTB_t, t, xn, xnB, ssq, ssqB, rr):
        P.act(xn, src_ap, AF.Square, (srcB,), (xnB, ssqB), accum=ssq[:, 0:1])
        P.act(ssq[:, 1:2], ssq[:, 0:1], AF.Ln, (ssqB, cB), (ssqB,), bias=epst[:, 0:1], scale=1.0 / D)
        P.act(ssq[:, 2:3], ssq[:, 1:2], AF.Exp, (ssqB,), (ssqB,), scale=-0.5)
        P.ts("dve", xn, src_ap, ssq[:, 2:3], None, ALU.mult, None, (srcB, ssqB), (xnB,))
        tpB = bankb[0]
        for dc in range(16):
            P.tr(PSB[:, dc * 128:(dc + 1) * 128], xn[:, dc * 128:(dc + 1) * 128], ident, (xnB, cB),
                 (bankb[0], bankb[1]), inc=(dc == 15))
        g3 = par[:, PC[gname]:PC[gname] + 16].unsqueeze(2).to_broadcast([128, 16, 128])
        P.tt("dve", hT[:, :, t * 128:(t + 1) * 128], PSB.rearrange("p (a b) -> p a b", b=128), g3, ALU.mult,
             (bankb[0], bankb[1], cB), (hTB_t,))

    def job_gen(jn, S, NOWN):
        j = J[jn]
        NCK = S // CH
        NOC = NOWN // CH
        NSEG = S // NOWN
        SEGW = NOWN // CH
        mcol0 = 0 if NSEG == 4 else 4
        x_d, rope_d = j["x"], j["rope"]

        UBk = [[Buf(f"U{k_}_{b_}") for b_ in range(8)] for k_ in range(S // CH)]
        UhL = [Buf(f"UhL{b_}") for b_ in range(8)]
        UhR = [Buf(f"UhR{b_}") for b_ in range(8)]
        LR = {}

        def lru_setup(n_u, n_tp, banks):
            if "wrg" not in LR:
                LR["wrg"] = wrg_c
                LR["wB2"] = wB2_c
                LR["carry"] = carry_c
                LR["carB"] = [[Buf(f"carry{d_}_{b_}") for b_ in range(8)] for d_ in range(2)]
                P.memset("dve", carry_c, 0.0, [bb_ for l_ in LR["carB"] for bb_ in l_])
            LR["ub"] = Pool([A.f32(CH + 3) for _ in range(n_u)], track=True)
            LR["cpp"] = Pool([A.f32(CH) for _ in range(n_u)], track=True)
            LR["xcbp"] = Pool([A.bf16(CH) for _ in range(n_u)], track=True)
            LR["tp"] = Pool([A.f32(CH) for _ in range(n_tp)], track=True)
            LR["lpp"] = Pool([bank[b] for b in banks])
            LR["lpp"].bufs = [bankb[b] for b in banks]

        NCK = S // CH
        NOC = NOWN // CH
        NSEG = S // NOWN
        SEGW = NOWN // CH
        mcol0 = 0 if NSEG == 4 else 4
        HfB = [Buf(f"Hf{w}") for w in range(NOC)]
        MIXB = j["MIXB"] = [[Buf(f"MIX{g}_{w}") for w in range(NOC)] for g in range(16)]
        U = j["U"]

        def lru_unit_outer(w, d_, blks):
            ub, cpp, xcbp, tp, lpp = LR["ub"], LR["cpp"], LR["xcbp"], LR["tp"], LR["lpp"]
            wrg, wB2, carry, carB = LR["wrg"], LR["wB2"], LR["carry"], LR["carB"]
            def mcol(jb):
                return msk[:, mcol0 + jb - 1: mcol0 + jb]

            def lru_unit(w, d_, blks):
                own = w < NOC
                us, xcs, xcbs, gates, rs, is_, as_ = {}, {}, {}, {}, {}, {}, {}
                held = []
                for blk in blks:
                    u, uB = ub.get(); held.append((ub, uB))
                    deps_ = [UBk[w][blk], UBk[w - 1][blk] if w > 0 else UhL[blk],
                             UBk[w + 1][blk] if w + 1 < NCK else UhR[blk]]
                    P.dma("sp", u, U[blk, :, CH * w: CH * w + CH + 3], deps_, (uB,))
                    if (w % SEGW) == 0:
                        jb = w // SEGW if w > 0 else NSEG
                        P.ts("dve", u[:, 0:1], u[:, 0:1], mcol(jb), None, ALU.mult, None, (uB, cB), (uB,))
                    if ((w + 1) % SEGW) == 0:
                        jb = (w + 1) // SEGW
                        P.ts("dve", u[:, CH + 1:CH + 3], u[:, CH + 1:CH + 3], mcol(jb), None, ALU.mult, None,
                             (uB, cB), (uB,))
                    us[blk] = (u, uB)
                yield
                for blk in blks:
                    u, uB = us[blk]
                    cw = PC["conv_w"] + blk * 4
                    xc, xcB = cpp.get(); held.append((cpp, xcB))
                    P.act(xc, u[:, 0:CH], AF.Identity, (uB, cB), (xcB,), bias=pcol("conv_b", blk),
                          scale=par[:, cw:cw + 1])
                    xcs[blk] = (xc, xcB)
                yield
                for blk in blks:
                    u, uB = us[blk]
                    cw = PC["conv_w"] + blk * 4
                    xc, xcB = xcs[blk]
                    for tap in range(1, 4):
                        P.stt("dve", xc, u[:, tap:tap + CH], par[:, cw + tap:cw + tap + 1], xc, ALU.mult, ALU.add,
                              (uB, cB, xcB), (xcB,))
                yield
                for blk in blks:
                    xc, xcB = xcs[blk]
                    xcb, xcbB = xcbp.get(); held.append((xcbp, xcbB))
                    P.copy("act", xcb, xc, (xcB,), (xcbB,))
                    xcbs[blk] = (xcb, xcbB)
                yield
                for blk in blks:
                    xcb, xcbB = xcbs[blk]
                    rhs = xcb if d_ == 0 else xcb[:, ::-1]
                    br, brB = lpp.get()
                    P.mm(br, wrg[:, 0, d_, blk, :], rhs, True, True, (wB2, xcbB), (brB,))
                    bi, biB = lpp.get()
                    P.mm(bi, wrg[:, 1, d_, blk, :], rhs, True, True, (wB2, xcbB), (biB,))
                    r_, rB = tp.get(); held.append((tp, rB))
                    P.act(r_, br, AF.Sigmoid, (brB, cB), (rB,), bias=pcol("b_r", d_ * 8 + blk))
                    rs[blk] = (r_, rB)
                    i_, iB = tp.get(); held.append((tp, iB))
                    P.act(i_, bi, AF.Sigmoid, (biB, cB), (iB,), bias=pcol("b_i", d_ * 8 + blk))
                    is_[blk] = (i_, iB)
                yield
                for blk in blks:
                    r_, rB = rs[blk]
                    a_, aB = tp.get(); held.append((tp, aB))
                    P.act(a_, r_, AF.Exp, (rB, cB), (aB,), scale=clam[:, d_ * 8 + blk: d_ * 8 + blk + 1])
                    as_[blk] = (a_, aB)
                yield
                for blk in blks:
                    r_, rB = rs[blk]
                    a_, aB = as_[blk]
                    P.tt("pool", r_, a_, a_, ALU.mult, (aB,), (rB,))
                yield
                for blk in blks:
                    r_, rB = rs[blk]
                    P.act(r_, r_, AF.Sqrt, (rB, cB), (rB,), bias=one_t[:, 0:1], scale=-1.0)
                yield
                for blk in blks:
                    xc, xcB = xcs[blk]
                    xcv = xc if d_ == 0 else xc[:, ::-1]
                    r_, rB = rs[blk]
                    i_, iB = is_[blk]
                    P.tt("pool", i_, i_, xcv, ALU.mult, (iB, xcB), (iB,))
                    P.tt("pool", i_, i_, r_, ALU.mult, (iB, rB), (iB,))
                yield
                hs = {}
                for blk in blks:
                    i_, iB = is_[blk]
                    a_, aB = as_[blk]
                    cc = carry[:, d_ * 8 + blk: d_ * 8 + blk + 1]
                    ccB = carB[d_][blk]
                    if d_ == 0 and (w % SEGW) == 0:
                        jb = w // SEGW if w > 0 else NSEG
                        P.ts("dve", cc, cc, mcol(jb), None, ALU.mult, None, (ccB, cB), (ccB,))
                    if d_ == 1 and ((w + 1) % SEGW) == 0:
                        jb = (w + 1) // SEGW
                        P.ts("dve", cc, cc, mcol(jb), None, ALU.mult, None, (ccB, cB), (ccB,))
                    h_, hB = rs[blk]
                    P.scan(h_, a_, i_, cc, (aB, iB, ccB), (hB,))
                    P.copy("dve", cc, h_[:, CH - 1:CH], (hB,), (ccB,))
                    hs[blk] = (h_, hB)
                yield
                for blk in blks:
                    h_, hB = hs[blk]
                    if own and d_ == 0:
                        P.dma("sp", j["Hf"][blk, :, CH * w: CH * w + CH], h_, (hB,), (HfB[w],))
                    if own and d_ == 1:
                        hf, hfB = tp.get(); held.append((tp, hfB))
                        P.dma("sp", hf, j["Hf"][blk, :, CH * w: CH * w + CH], (HfB[w],), (hfB,))
                        gt, gtB = tp.get(); held.append((tp, gtB))
                        P.dma("sp", gt, j["G"][blk, :, CH * w: CH * w + CH], (j["GB"][w],) if "GB" in j else (), (gtB,))
                        P.tt("pool", hf, hf, h_[:, ::-1], ALU.add, (hfB, hB), (hfB,))
                        P.tt("pool", hf, hf, gt, ALU.mult, (hfB, gtB), (hfB,))
                        P.dma("sp", j["MIX"][8 + blk, :, CH * w: CH * w + CH], hf, (hfB,), (MIXB[8 + blk][w],))
                for pl_, b_ in held:
                    pl_.rel(b_)


            yield from lru_unit(w, d_, blks)

        def lockstep(gens, skew=0):
            active = list(enumerate(gens))
            t_ = 0
            while active:
                for item in list(active):
                    j_, g_ = item
                    if t_ < skew * j_:
                        continue
                    try:
                        next(g_)
                    except StopIteration:
                        active.remove(item)
                t_ += 1

        if "1" in stages:
            A.top = mark0
            wuq = A.bf16(4 * 1536).rearrange("p (c n) -> p c n", c=4)
            wuqs = A.bf16(4 * 512).rearrange("p (c n) -> p c n", c=4)
            wk = A.bf16(2 * 1024).rearrange("p (c h d) -> p c h d", c=2, h=8)
            wv = A.bf16(2 * 1024).rearrange("p (c h d) -> p c h d", c=2, h=8)
            wB = Buf("wres")
            P.dma("pq", wuq, w_uq.rearrange("(c p) n -> p c n", p=128), (), (wB,))
            P.dma("pq", wuqs, w_uqsw.rearrange("(c p) n -> p c n", p=128), (), (wB,))
            ukv_v = w_ukv.rearrange("(c p) (h two d) -> p c h two d", p=128, two=2, d=128)
            for c in range(2):
                P.dma("pq", wk[:, c], ukv_v[:, c, :, 0, :], (), (wB,))
                P.dma("pq", wv[:, c], ukv_v[:, c, :, 1, :], (), (wB,))
            ring = Pool([A.bf16(8192) for _ in range(2)])
            xt = Pool([A.f32(D) for _ in range(3)])
            xn4 = A.bf16(4 * D).rearrange("p (t d) -> p t d", t=4)
            xn4B = [Buf(f"xn{t}") for t in range(4)]
            ssqp = Pool([A.f32(4) for _ in range(4)])
            hT2 = [A.bf16(16 * CH).rearrange("p (a b) -> p a b", a=16) for _ in range(2)]
            hTB2 = [[Buf(f"hT{p_}_{t}") for t in range(4)] for p_ in range(2)]
            cf = A.f32(4 * CH).rearrange("p (a b) -> p a b", a=4)
            cfB = Buf("cf")
            cqn2 = [A.bf16(4 * CH).rearrange("p (a b) -> p a b", a=4) for _ in range(2)]
            cqnB2 = [Buf("cqn0"), Buf("cqn1")]
            ckvn2 = [A.bf16(2 * CH).rearrange("p (a b) -> p a b", a=2) for _ in range(2)]
            ckvnB2 = [Buf("ckvn0"), Buf("ckvn1")]
            kpf = A.f32(CH)
            kpsf = A.f32(CH)
            kptB = Buf("kptmp")
            kpr2 = [A.f32(CH) for _ in range(2)]
            sqkp2 = [A.bf16(CH) for _ in range(2)]
            kpB2 = [Buf("kp0"), Buf("kp1")]
            cst2 = [A.f32(2 * CH).rearrange("p (a b) -> p a b", a=2) for _ in range(2)]
            cstB2 = [Buf("cs0"), Buf("cs1")]
            vbuf = A.bf16(8 * 4 * 128).rearrange("p (h t d) -> p h t d", h=8, t=4)
            vB = Buf("vbuf")
            sqp_m = Pool([A.bf16(CH) for _ in range(4)])
            f32p_m = Pool([A.f32(CH) for _ in range(3)])
            sqp_h = Pool([A.bf16(CH) for _ in range(3)])
            f32p_h = Pool([A.f32(CH) for _ in range(3)])
            b16p_h = Pool([A.bf16(CH) for _ in range(4)])
            mp = Pool([bank[2], bank[3], bank[0], bank[1]])
            mp.bufs = [bankb[2], bankb[3], bankb[0], bankb[1]]
            hp = Pool([bank[b] for b in range(4, 8)])
            hp.bufs = [bankb[b] for b in range(4, 8)]

            order = list(range(NCK))
            seq = []
            for k in order:
                seq += [0, 1, 2] + ([3, 4, 5] if k < NOC else [])
            SS = SlabStream(ring, seq)
            st_ = {"sidx": 0}
            side = []

            def tick(n=1):
                for _ in range(n):
                    for g_ in list(side):
                        try:
                            next(g_)
                        except StopIteration:
                            side.remove(g_)

            def next_slab():
                r_ = SS.get(st_["sidx"])
                st_["sidx"] += 1
                return r_

            def proj16(slab, col0, m, hT, hTB, slB):
                tick(3)
                bk, bb = mp.get()
                for dc in range(16):
                    P.mm(bk[0:m], slab[:, dc, col0:col0 + m], hT[:, dc, :], dc == 0, dc == 15, (slB, *hTB), (bb,))
                return bk, bb

            xt_cur = {}

            def xprep_load(k, ts=(0, 1, 2)):
                c0 = k * CH
                for t in ts:
                    xt_t, xtB = xt.get()
                    P.dma("pq", xt_t, x_d[c0 + t * 128: c0 + (t + 1) * 128, :], (), (xtB,))
                    xt_cur[(k, t)] = (xt_t, xtB)

            def xprep_elem(k):
                for t in range(4):
                    if t == 1:
                        xprep_load(k, (3,))
                    xt_t, xtB = xt_cur.pop((k, t))
                    ssq, ssqB = ssqp.get()
                    xn = xn4[:, t, :]
                    P.act(xn, xt_t, AF.Square, (xtB,), (xn4B[t], ssqB), accum=ssq[:, 0:1])
                    P.act(ssq[:, 1:2], ssq[:, 0:1], AF.Ln, (ssqB, cB), (ssqB,), bias=epst[:, 0:1], scale=1.0 / D)
                    P.act(ssq[:, 2:3], ssq[:, 1:2], AF.Exp, (ssqB,), (ssqB,), scale=-0.5)
                    P.ts("dve", xn, xt_t, ssq[:, 2:3], None, ALU.mult, None, (xtB, ssqB), (xn4B[t],))

            def xprep_tr(par_):
                hT, hTB = hT2[par_], hTB2[par_]
                g3 = par[:, PC["g_attn"]:PC["g_attn"] + 16].unsqueeze(2).to_broadcast([128, 16, 128])
                for t in range(4):
                    for dc in range(16):
                        P.tr(PSB[:, dc * 128:(dc + 1) * 128], xn4[:, t, dc * 128:(dc + 1) * 128], ident, (xn4B[t], cB),
                             (bankb[0], bankb[1]), inc=(dc == 15))
                    P.tt("dve", hT[:, :, t * 128:(t + 1) * 128], PSB.rearrange("p (a b) -> p a b", b=128), g3, ALU.mult,
                         (bankb[0], bankb[1], cB), (hTB[t],))

            def nf_part1(ngrp, slab, slB, hT, hTB):
                sqs = []
                for g in range(ngrp):
                    bk, bb = proj16(slab, g * 128, 128, hT, hTB, slB)
                    P.copy("act", cf[:, g, :], bk, (bb,), (cfB,))
                    sq, sqB = sqp_m.get()
                    P.act(sq, bk, AF.Square, (bb,), (sqB,))
                    sqs.append((sq, sqB))
                return sqs

            def nf_part2(sqs, gname, width, outn, outB):
                ngrp = len(sqs)
                sb, sbB = mp.get()
                for g, (sq, sqB) in enumerate(sqs):
                    P.mm(sb, ones, sq, g == 0, g == ngrp - 1, (cB, sqB), (sbB,))
                rt, rtB = f32p_m.get()
                rstd_from_bank(sb, sbB, 1.0 / width, rt, rtB)
                for g in range(ngrp):
                    P.stt("dve", outn[:, g, :], cf[:, g, :], pcol(gname, g), rt, ALU.mult, ALU.mult,
                          (cfB, rtB, cB), (outB,))

            def qk_head(wn_fn, wp_fn, wps_fn, nc_, src, srcB, shared, cst, cstB, gn, gp, gps, dst_n, dst_p, c0):
                bn, bnB = hp.get()
                for c in range(nc_):
                    P.mm(bn, wn_fn(c), src[:, c, :], c == 0, c == nc_ - 1, (wB, srcB), (bnB,))
                sq, sqB = sqp_h.get()
                P.act(sq, bn, AF.Square, (bnB,), (sqB,))
                if shared is None:
                    bp, bpB = hp.get()
                    for c in range(nc_):
                        P.mm(bp[0:64], wp_fn(c), src[:, c, :], c == 0, c == nc_ - 1, (wB, srcB), (bpB,))
                    bs, bsB = hp.get()
                    for c in range(nc_):
                        P.mm(bs[0:64], wps_fn(c), src[:, c, :], c == 0, c == nc_ - 1, (wB, srcB), (bsB,))
                    sq2, sq2B = sqp_h.get()
                    P.act(sq2[0:64], bp[0:64], AF.Square, (bpB,), (sq2B,))
                else:
                    kpr, sq2, sq2B = shared
                yield
                sb, sbB = hp.get()
                P.mm(sb, ones, sq, True, False, (cB, sqB), (sbB,))
                P.mm(sb, ones[0:64, :], sq2[0:64], False, True, (cB, sq2B), (sbB,))
                rt, rtB = f32p_h.get()
                rstd_from_bank(sb, sbB, 1.0 / 192.0, rt, rtB)
                yield
                on, onB = b16p_h.get()
                P.stt("dve", on, bn, gn, rt, ALU.mult, ALU.mult, (bnB, rtB, cB), (onB,))
                P.dma("sp", dst_n[:, c0:c0 + CH], on, (onB,), ())
                op_, opB = b16p_h.get()
                if shared is None:
                    t1, t1B = f32p_h.get()
                    t2, t2B = f32p_h.get()
                    P.stt("dve", t1[0:64], bp[0:64], gp, cst[0:64, 0, :], ALU.mult, ALU.mult, (bpB, cstB, cB), (t1B,))
                    P.stt("dve", t2[0:64], bs[0:64], gps, cst[0:64, 1, :], ALU.mult, ALU.mult, (bsB, cstB, cB), (t2B,))
                    P.tt("dve", t1[0:64], t1[0:64], t2[0:64], ALU.add, (t1B, t2B), (t1B,))
                    P.tt("dve", op_[0:64], t1[0:64], rt[0:64], ALU.mult, (t1B, rtB), (opB,))
                else:
                    P.tt("dve", op_[0:64], kpr[0:64], rt[0:64], ALU.mult, (sq2B, rtB), (opB,))
                P.dma("sp", dst_p[:, c0:c0 + CH], op_[0:64], (opB,), ())
                yield

            def heads(k, par_):
                own = k < NOC
                c0 = k * CH
                ckvn, ckvnB = ckvn2[par_], ckvnB2[par_]
                cst, cstB = cst2[par_], cstB2[par_]
                for h in range(NH):
                    yield from qk_head(lambda c, h=h: wk[:, c, h, :], None, None, 2, ckvn, ckvnB,
                                       (kpr2[par_], sqkp2[par_], kpB2[par_]), cst, cstB,
                                       pcol("g_kn"), None, None, j["Kn"][h], j["Kp"][h], c0)
                for t in range(4):
                    for jj in range(2):
                        bk, bb = hp.get()
                        for c in range(2):
                            P.mm(bk, ckvn[:, c, t * 128:(t + 1) * 128], wv[:, c, 4 * jj:4 * jj + 4, :],
                                 c == 0, c == 1, (ckvnB, wB), (bb,))
                        P.copy("act" if jj == 0 else "dve", vbuf[:, 4 * jj:4 * jj + 4, t, :],
                               bk.rearrange("p (h d) -> p h d", h=4), (bb,), (vB,))
                    yield
                for h in range(NH):
                    P.dma("sp", j["V"][h, :, 4 * k:4 * k + 4, :], vbuf[:, h], (vB,), ())
                yield
                if own:
                    cqn, cqnB = cqn2[par_], cqnB2[par_]
                    for h in range(NH):
                        yield from qk_head(lambda c, h=h: wuq[:, c, h * 192:h * 192 + 128],
                                           lambda c, h=h: wuq[:, c, h * 192 + 128:h * 192 + 192],
                                           lambda c, h=h: wuqs[:, c, h * 64:(h + 1) * 64], 4, cqn, cqnB, None, cst, cstB,
                                           gqs[:, 0:1], gqs[0:64, 1:2], gqs[0:64, 2:3], j["Qn"][h], j["Qp"][h], c0)

            GB = j["GB"] = [Buf(f"G{k}") for k in range(NOC)]

            xprep_load(order[0])
            xprep_elem(order[0])
            xprep_tr(0)
            if len(order) > 1:
                xprep_load(order[1])
            emit_s0b()
            for i, k in enumerate(order):
                par_ = i % 2
                own = k < NOC
                c0 = k * CH
                hT, hTB = hT2[par_], hTB2[par_]
                cst, cstB = cst2[par_], cstB2[par_]
                kpr, sqkp, kpB = kpr2[par_], sqkp2[par_], kpB2[par_]
                if i + 1 < len(order) and i > 0:
                    xprep_load(order[i + 1])
                P.dma("sp", cst[0:64], rope_d[:, :, c0:c0 + CH].rearrange("a p n -> p a n"), (), (cstB,))
                sl, slB = next_slab()
                sq_ckv = nf_part1(2, sl, slB, hT, hTB)
                bk, bb = proj16(sl, 256, 64, hT, hTB, slB)
                P.copy("act", kpf[0:64], bk[0:64], (bb,), (kptB,))
                P.act(sqkp[0:64], bk[0:64], AF.Square, (bb,), (kpB,))
                bk, bb = proj16(sl, 320, 64, hT, hTB, slB)
                P.copy("act", kpsf[0:64], bk[0:64], (bb,), (kptB,))
                P.stt("dve", kpr[0:64], kpf[0:64], pcol("g_kp", 0, 1, 64), cst[0:64, 0, :], ALU.mult, ALU.mult,
                      (kptB, cstB, cB), (kpB,))
                P.stt("dve", kpsf[0:64], kpsf[0:64], pcol("g_kps", 0, 1, 64), cst[0:64, 1, :], ALU.mult, ALU.mult,
                      (kptB, cstB, cB), (kptB,))
                P.tt("dve", kpr[0:64], kpr[0:64], kpsf[0:64], ALU.add, (kpB, kptB), (kpB,))
                for s_ in range(2):
                    sl, slB = next_slab()
                    for g in range(4):
                        bk, bb = proj16(sl, g * 128, 128, hT, hTB, slB)
                        ft, ftB = f32p_m.get()
                        P.copy("act" if g % 2 == 0 else "dve", ft, bk, (bb,), (ftB,))
                        blk = s_ * 4 + g
                        P.dma("sp", j["U"][blk, :, 1 + c0: 1 + c0 + CH], ft, (ftB,), (UBk[k][blk],))
                        if k == 0:
                            P.dma("sp", j["U"][blk, :, 1 + S: 1 + S + 2], ft[:, 0:2], (ftB,), (UhR[blk],))
                        if k == NCK - 1:
                            P.dma("sp", j["U"][blk, :, 0:1], ft[:, CH - 1:CH], (ftB,), (UhL[blk],), slow=True)
                    if s_ == 0:
                        nf_part2(sq_ckv, "g_kva", 256.0, ckvn2[par_], ckvnB2[par_])
                        if i + 1 < len(order):
                            xprep_elem(order[i + 1])
                if own:
                    sl, slB = next_slab()
                    sq_cq = nf_part1(4, sl, slB, hT, hTB)
                    for s_ in range(2):
                        sl, slB = next_slab()
                        for g in range(4):
                            bk, bb = proj16(sl, g * 128, 128, hT, hTB, slB)
                            ft, ftB = f32p_m.get()
                            g2, g2B = f32p_m.get()
                            P.act(g2, bk, AF.Square, (bb,), (g2B,))
                            P.ts("dve", g2, g2, 0.044715, 1.0, ALU.mult, ALU.add, (g2B,), (g2B,))
                            P.tt("dve", g2, g2, bk, ALU.mult, (g2B, bb), (g2B,))
                            P.act(g2, g2, AF.Sigmoid, (g2B,), (g2B,), scale=float(2.0 * np.sqrt(2.0 / np.pi)))
                            P.tt("dve", ft, g2, bk, ALU.mult, (g2B, bb), (ftB,))
                            P.dma("sp", j["G"][s_ * 4 + g, :, c0:c0 + CH], ft, (ftB,), (GB[k],))
                        if s_ == 0:
                            nf_part2(sq_cq, "g_qa", 512.0, cqn2[par_], cqnB2[par_])
                while side:
                    tick()
                if i + 1 < len(order):
                    xprep_tr(1 - par_)
                side.append(heads(k, par_))
            while side:
                tick()
            P.barrier()
        yield

        if "2" in stages:
            A.top = mark0
            lru_setup(16, 48, list(range(8)))

            def lru_stream(ws, d_, half):
                for w in ws:
                    yield from lru_unit_outer(w, d_, [4 * half + b_ for b_ in range(4)])

            wf_, wb_ = list(range(NOC, NCK)), list(range(NCK - 1, NOC - 1, -1))
            lockstep([lru_stream(wf_, 0, 0), lru_stream(wb_, 1, 0), lru_stream(wf_, 0, 1), lru_stream(wb_, 1, 1)], skew=3)
            for w in range(NOC):
                lockstep([lru_unit_outer(w, 0, [0, 1, 2, 3]), lru_unit_outer(w, 0, [4, 5, 6, 7])], skew=5)
            for w in range(NOC - 1, -1, -1):
                lockstep([lru_unit_outer(w, 1, [0, 1, 2, 3]), lru_unit_outer(w, 1, [4, 5, 6, 7])], skew=5)
            P.barrier()
        yield

        if "3" in stages:
            A.top = mark0
            QB = 1024
            NQB = NOWN // QB
            KG = 2048 if S >= 2048 else S
            NT = KG // 128
            NKG = S // KG
            knp = Pool([A.bf16(KG) for _ in range(3)])
            kpp = Pool([A.bf16(KG) for _ in range(3)])
            vp = Pool([A.bf16(KG).rearrange("p (t d) -> p t d", d=128) for _ in range(3)])
            qnp = Pool([A.bf16(QB) for _ in range(2)])
            qpp = Pool([A.bf16(QB) for _ in range(2)])
            ptp = Pool([A.bf16(QB) for _ in range(4)])
            outp = Pool([A.f32(CH) for _ in range(4)])
            sp_ = Pool([PS[:, (2 + 2 * p) * 512:(4 + 2 * p) * 512] for p in range(3)])
            if "MIXB" not in j:
                j["MIXB"] = [[Buf() for w in range(NOC)] for g in range(16)]
            dacc = Pool([(A.f32(QB), A.f32(QB)) for _ in range(2)])
            dacc.bufs = [(Buf(), Buf()) for _ in range(2)]
            ob = [bank[0], bank[1]]
            obB = [bankb[0], bankb[1]]
            LA = 2
            for h in range(NH):
                for qb in range(NQB):
                    qn, qnB = qnp.get()
                    qp, qpB = qpp.get()
                    P.dma("sp", qn, j["Qn"][h, :, qb * QB:(qb + 1) * QB], (), (qnB,))
                    P.dma("sp", qp[0:64], j["Qp"][h, :, qb * QB:(qb + 1) * QB], (), (qpB,))
                    (accd, accq), (accdB, accqB) = dacc.get()
                    groups = {}

                    def ensure_group(kg):
                        if kg >= NKG or kg in groups:
                            return
                        kn, knB = knp.get()
                        kp, kpB_ = kpp.get()
                        vv, vvB = vp.get()
                        P.dma("sp", kn, j["Kn"][h, :, kg * KG:(kg + 1) * KG], (), (knB,))
                        P.dma("sp", kp[0:64], j["Kp"][h, :, kg * KG:(kg + 1) * KG], (), (kpB_,))
                        P.dma("sp", vv, j["V"][h, :, kg * NT:(kg + 1) * NT, :], (), (vvB,))
                        groups[kg] = (kn, knB, kp, kpB_, vv, vvB)

                    def tile(i):
                        kg, t = divmod(i, NT)
                        ensure_group(kg)
                        if t == 0:
                            ensure_group(kg + 1)
                        kn, knB, kp, kpB_, vv, vvB = groups[kg]
                        return (kn[:, t * 128:(t + 1) * 128], knB, kp[0:64, t * 128:(t + 1) * 128], kpB_,
                                vv[:, t, :], vvB)

                    n = NKG * NT
                    stq = {}

                    def emit_qk(i):
                        knt, knB, kpt, kpB_, vt, vvB = tile(i)
                        st, stB = sp_.get()
                        for hf_ in range(2):
                            P.mm(st[:, hf_ * 512:(hf_ + 1) * 512], knt, qn[:, hf_ * 512:(hf_ + 1) * 512], True, False,
                                 (knB, qnB), (stB,), inc=False)
                        for hf_ in range(2):
                            P.mm(st[:, hf_ * 512:(hf_ + 1) * 512], kpt, qp[0:64, hf_ * 512:(hf_ + 1) * 512], False, True,
                                 (kpB_, qpB), (stB,), inc=(hf_ == 1))
                        pt, ptB = ptp.get()
                        P.act(pt, st, AF.Exp, (stB,), (ptB,))
                        stq[i] = (pt, ptB, vt, vvB)

                    for i in range(min(LA, n)):
                        emit_qk(i)
                    cnt = {"dve": 0, "pool": 0}
                    for i in range(n):
                        if i + LA < n:
                            emit_qk(i + LA)
                        pt, ptB, vt, vvB = stq.pop(i)
                        for hf_ in range(2):
                            P.mm(ob[hf_], vt, pt[:, hf_ * 512:(hf_ + 1) * 512], i == 0, i == n - 1, (vvB, ptB), (obB[hf_],),
                                 inc=(hf_ == 1))
                        en_ = "pool" if i % 3 == 2 else "dve"
                        ac_, acB_ = (accd, accdB) if en_ == "dve" else (accq, accqB)
                        if cnt[en_] == 0:
                            P.copy(en_, ac_, pt, (ptB,), (acB_,))
                        else:
                            P.tt(en_, ac_, ac_, pt, ALU.add, (acB_, ptB), (acB_,))
                        cnt[en_] += 1
                    if cnt["pool"]:
                        P.tt("dve", accd, accd, accq, ALU.add, (accdB, accqB), (accdB,))
                    db, dbB = sp_.get()
                    for hf_ in range(2):
                        P.mm(db[:, hf_ * 512:(hf_ + 1) * 512], ones32, accd[:, hf_ * 512:(hf_ + 1) * 512], True, True,
                             (cB, accdB), (dbB,), inc=(hf_ == 1))
                    for hf_ in range(2):
                        rd, rdB = outp.get()
                        P.recip(rd, db[:, hf_ * 512:(hf_ + 1) * 512], (dbB,), (rdB,))
                        P.tt("dve", rd, ob[hf_], rd, ALU.mult, (obB[hf_], rdB), (rdB,))
                        c0_ = qb * QB + hf_ * 512
                        P.dma("sp", j["MIX"][h, :, c0_:c0_ + 512], rd, (rdB,), (j["MIXB"][h][c0_ // CH],))
            P.barrier()
        yield

        if "4" in stages:
            emit_s0b()
            A.top = mark0
            ring = Pool([A.bf16(8192) for _ in range(3)])
            xres = [A.f32(D) for _ in range(4)]
            xrB = [Buf(f"xres{t}") for t in range(4)]
            xn = A.bf16(D)
            xnB = Buf("xn4")
            ssq = A.f32(4)
            ssqB = Buf("ssq4")
            hT = A.bf16(16 * CH).rearrange("p (a b) -> p a b", a=16)
            hTB = [Buf(f"h2T{t}") for t in range(4)]
            mixed2 = [A.bf16(16 * CH).rearrange("p (a b) -> p a b", a=16) for _ in range(2)]
            mixedB2 = [Buf("mixed0"), Buf("mixed1")]
            big = A.bf16(NFC * CH)
            actT = big.rearrange("p (a b) -> p a b", a=NFC)
            bigB = Buf("big")
            stg = hT.rearrange("p a b -> p (a b)").bitcast(F32).rearrange("p (a b) -> p a b", a=8)
            pp23 = Pool([bank[2], bank[3]])
            pp23.bufs = [bankb[2], bankb[3]]
            sqp = Pool([A.bf16(CH) for _ in range(8)])
            f32p = Pool([A.f32(CH) for _ in range(4)])
            pp = Pool([bank[b] for b in range(2, 8)])
            pp.bufs = [bankb[b] for b in range(2, 8)]
            seq = []
            for k in range(NOC):
                seq += [SL_OUT + c for c in range(4)] + [SL_GU + s for s in range(22)] + [SL_DN + s for s in range(16)]
            SS = SlabStream(ring, seq)
            sidx = 0
            MIXB = j.get("MIXB") or [[Buf() for w in range(NOC)] for g in range(16)]
            def mix_prep(k):
                c0 = k * CH
                mixed, mixedB = mixed2[k % 2], mixedB2[k % 2]
                for half in range(2):
                    for g in range(8):
                        gg = half * 8 + g
                        P.dma("sp", stg[:, g, :], j["MIX"][gg, :, c0:c0 + CH], (MIXB[gg][k],), hTB)
                    yield
                    sqs = []
                    for g in range(8):
                        sq, sqB = sqp.get()
                        P.act(sq, stg[:, g, :], AF.Square, hTB, (sqB,))
                        sqs.append((sq, sqB))
                    yield
                    sb, sbB = pp23.get()
                    for g, (sq, sqB) in enumerate(sqs):
                        P.mm(sb, ones, sq, g == 0, g == 7, (cB, sqB), (sbB,))
                    rt, rtB = f32p.get()
                    rstd_from_bank(sb, sbB, 1.0 / 1024.0, rt, rtB)
                    yield
                    gname = "g_ao" if half == 0 else "g_lo"
                    for g in range(8):
                        gg = half * 8 + g
                        P.stt("dve", mixed[:, gg, :], stg[:, g, :], pcol(gname, g), rt, ALU.mult, ALU.mult,
                              (*hTB, rtB, cB), (mixedB,))
                    yield

            for _ in mix_prep(0):
                pass
            for k in range(NOC):
                c0 = k * CH
                mixed, mixedB = mixed2[k % 2], mixedB2[k % 2]
                for t in range(4):
                    P.dma("pq", xres[t], x_d[c0 + t * 128:c0 + (t + 1) * 128, :], (), (xrB[t],))
                for cg in range(4):
                    sl, slB = SS.get(sidx); sidx += 1
                    for t in range(4):
                        bk, bb = pp.get()
                        for g in range(16):
                            P.mm(bk, mixed[:, g, t * 128:(t + 1) * 128], sl[:, g, :], g == 0, g == 15, (mixedB, slB), (bb,))
                        xs = xres[t][:, cg * 512:(cg + 1) * 512]
                        P.tt("dve", xs, bk, xs, ALU.add, (bb, xrB[t]), (xrB[t],))
                for t in range(4):
                    prep_hT(xres[t], xrB[t], "g_ffn", hT, hTB[t], t, xn, xnB, ssq, ssqB, None)
                for s in range(11):
                    sg, sgB = SS.get(sidx); sidx += 1
                    su, suB = SS.get(sidx); sidx += 1
                    for fc in range(4):
                        bg, bgB = pp.get()
                        for dc in range(16):
                            P.mm(bg, sg[:, dc, fc * 128:(fc + 1) * 128], hT[:, dc, :], dc == 0, dc == 15, (sgB, *hTB), (bgB,))
                        bu, buB = pp.get()
                        for dc in range(16):
                            P.mm(bu, su[:, dc, fc * 128:(fc + 1) * 128], hT[:, dc, :], dc == 0, dc == 15, (suB, *hTB), (buB,))
                        ft, ftB = f32p.get()
                        P.act(ft, bg, AF.Silu, (bgB,), (ftB,))
                        P.tt("dve", actT[:, s * 4 + fc, :], ft, bu, ALU.mult, (ftB, buB), (bigB,))
                accb = [bank[b] for b in range(4, 8)]
                accB = [bankb[b] for b in range(4, 8)]
                nxt = mix_prep(k + 1) if k + 1 < NOC else iter(())
                for cg in range(4):
                    for fg in range(4):
                        sl, slB = SS.get(sidx); sidx += 1
                        next(nxt, None)
                        for jf in range(11):
                            f = fg * 11 + jf
                            for t in range(4):
                                P.mm(accb[t], actT[:, f, t * 128:(t + 1) * 128], sl[:, jf, :], f == 0, f == NFC - 1,
                                     (bigB, slB), (accB[t],), inc=(f == NFC - 1) or (jf == 10 and t == 3))
                    for t in range(4):
                        xs = xres[t][:, cg * 512:(cg + 1) * 512]
                        P.tt("dve", xs, accb[t], xs, ALU.add, (accB[t], xrB[t]), (xrB[t],))
                for _ in nxt:
                    pass
                for t in range(4):
                    P.dma("sp", j["y"][c0 + t * 128:c0 + (t + 1) * 128, :], xres[t], (xrB[t],), ())
            P.barrier()
        yield

    gens_ = [job_gen(*jb_) for jb_ in jobs]
    while gens_:
        for g_ in list(gens_):
            try:
                next(g_)
            except StopIteration:
                gens_.remove(g_)
    P.barrier()
    with nc.Block() as block:
        @block.tensor
        def _(e):
            P.replay("pe", e)

        @block.scalar
        def _(e):
            P.replay("act", e)

        @block.vector
        def _(e):
            P.replay("dve", e)

        @block.gpsimd
        def _(e):
            P.replay("pool", e)

        @block.sync
        def _(e):
            P.replay("sp", e)
    return nc, P


def _pack_params(inp):
    p = np.zeros((128, NPAR), np.float32)

    def put(name, arr):
        p[:arr.shape[0], PC[name]:PC[name] + arr.shape[1]] = arr
    f = lambda a: np.asarray(a, np.float32)
    put("g_attn", f(inp["attn_norm_g"])[0].reshape(16, 128).T)
    put("g_ffn", f(inp["ffn_norm_g"])[0].reshape(16, 128).T)
    put("g_qa", f(inp["q_a_norm_g"])[0].reshape(4, 128).T)
    put("g_kva", f(inp["kv_a_norm_g"])[0].reshape(2, 128).T)
    sw = (np.arange(64) + 32) % 64
    gq = f(inp["q_norm_g"])[0]
    gk = f(inp["k_norm_g"])[0]
    put("g_qn", gq[:128, None])
    put("g_qp", gq[128:, None])
    put("g_qps", gq[128:][sw][:, None])
    put("g_kn", gk[:128, None])
    put("g_kp", gk[128:, None])
    put("g_kps", gk[128:][sw][:, None])
    cw = f(inp["conv_w"])[0]
    put("conv_w", cw.reshape(4, 8, 128).transpose(2, 1, 0).reshape(128, 32))
    put("conv_b", f(inp["conv_b"])[0].reshape(8, 128).T)
    put("b_r", f(inp["b_rg_r"])[0].reshape(16, 128).T)
    put("b_i", f(inp["b_rg_i"])[0].reshape(16, 128).T)
    put("lam", f(inp["lru_lambda"])[0].reshape(16, 128).T)
    put("g_ao", f(inp["attn_out_norm_g"])[0].reshape(8, 128).T)
    put("g_lo", f(inp["lru_out_norm_g"])[0].reshape(8, 128).T)
    return p


def _rope_table(pos):
    half = 32
    inv_freq = (np.float32(10000.0) ** (np.float32(-2.0) * np.arange(half, dtype=np.float32) / np.float32(64))).astype(np.float32)
    ang = (pos.astype(np.float32)[:, None] * inv_freq[None, :]).astype(np.float32)
    c = np.cos(ang.astype(np.float64)).astype(np.float32).T
    s = np.sin(ang.astype(np.float64)).astype(np.float32).T
    cosT = np.concatenate([c, c], 0)
    sinT = np.concatenate([-s, s], 0)
    return np.ascontiguousarray(np.stack([cosT, sinT], 0))


_CACHE = {}


def kernel(**inputs):
    f = lambda a: np.ascontiguousarray(np.asarray(a, np.float32))
    xp = f(inputs["x_prompt"])
    xs = f(inputs["x_sample"])
    cfg = {"jobs": [("p", 16384, 4096), ("s", 2048, 1024)], "debug": False}
    if "nc" not in _CACHE:
        _CACHE["nc"] = build(cfg)
    nc, P = _CACHE["nc"]
    params = _pack_params(inputs)
    sw = (np.arange(64) + 32) % 64
    w_in = f(inputs["w_in"])[0]
    w_pesw = np.ascontiguousarray(w_in[:, 768:832][:, sw])
    w_uq = f(inputs["w_uq"])[0]
    w_uqsw = np.ascontiguousarray(w_uq.reshape(512, 8, 192)[:, :, 128:][:, :, sw].reshape(512, 512))
    w_rg = np.ascontiguousarray(np.stack([f(inputs["w_rg_r"])[0], f(inputs["w_rg_i"])[0]], 0))
    common = {
        "w_in": w_in, "w_pesw": w_pesw, "w_uq": w_uq, "w_uqsw": w_uqsw, "w_ukv": f(inputs["w_ukv"])[0],
        "w_rg": w_rg, "w_out": f(inputs["w_out"])[0], "w_gate": f(inputs["w_gate"])[0], "w_up": f(inputs["w_up"])[0],
        "w_down": f(inputs["w_down"])[0], "ident": np.eye(128, dtype=np.float32), "params": params,
    }
    in_maps = []
    for c in range(8):
        pb, pq = c // 4, c % 4
        sb, sh = c // 2, c % 2
        m = dict(common)
        m["xp"] = np.ascontiguousarray(np.roll(xp[pb], -pq * 4096, axis=0))
        m["xs"] = np.ascontiguousarray(np.roll(xs[sb], -sh * 1024, axis=0))
        m["ropep"] = _rope_table((np.arange(16384) + pq * 4096) % 16384)
        m["ropes"] = _rope_table((np.arange(2048) + sh * 1024) % 2048)
        mk = np.ones((128, 8), np.float32)
        mk[:, (4 - pq) - 1] = 0.0
        mk[:, 4 + (2 - sh) - 1] = 0.0
        m["msk"] = mk
        in_maps.append(m)
    res = run_bass_kernel_spmd(nc, in_maps, core_ids=list(range(8)))
    yp = np.zeros((2, 16384, D), np.float32)
    ys = np.zeros((4, 2048, D), np.float32)
    for c in range(8):
        pb, pq = c // 4, c % 4
        sb, sh = c // 2, c % 2
        yp[pb, pq * 4096:(pq + 1) * 4096] = res.results[c]["yp"]
        ys[sb, sh * 1024:(sh + 1) * 1024] = res.results[c]["ys"]
    return (yp, ys)
```

```python
import os
import numpy as np
import concourse.bass as bass
import concourse.mybir as mybir
from concourse.bass_utils import run_bass_kernel_spmd

F32 = mybir.dt.float32
BF16 = mybir.dt.bfloat16
ALU = mybir.AluOpType
AF = mybir.ActivationFunctionType

D = 2048
NH = 8
DFF = 5632
NFC = DFF // 128
EPS = 1e-6
CH = 512
NQD = 16

PC = {}
_o = 0
for _n, _w in [("g_attn", 16), ("g_ffn", 16), ("g_qa", 4), ("g_kva", 2), ("g_qn", 1), ("g_qp", 1), ("g_qps", 1),
               ("g_kn", 1), ("g_kp", 1), ("g_kps", 1), ("conv_w", 32), ("conv_b", 8), ("b_r", 16), ("b_i", 16),
               ("lam", 16), ("g_ao", 8), ("g_lo", 8)]:
    PC[_n] = _o
    _o += _w
NPAR = _o


class Buf:
    __slots__ = ("name", "w", "r")

    def __init__(self, name=""):
        self.name = name
        self.w = None
        self.r = {}


class Eng:
    def __init__(self, name, is_dma=False):
        self.name = name
        self.is_dma = is_dma
        self.ops = []
        self.seen = {}
        self.count = 0
        self.sem = None
        self.qsems = []
        self.qcnt = []
        self.rr = 0
        self.pending = []


class Prog:
    def __init__(self, nc):
        self.nc = nc
        self.sems = []
        self.eng = {}
        for n in ("pe", "act", "dve", "pool"):
            e = Eng(n)
            e.sem = self._newsem(n)
            self.eng[n] = e
        for n in ("sp", "pq"):
            e = Eng(n, True)
            e.qsems = [self._newsem(f"{n}{i}") for i in range(NQD)]
            e.qcnt = [0] * NQD
            self.eng[n] = e
        self.n_ops = 0

    def _newsem(self, name):
        self.sems.append(self.nc.alloc_semaphore("s_" + name))
        return len(self.sems) - 1

    def _stream(self, en):
        return self.eng["pool"] if en == "pq" else self.eng[en]

    def _waits(self, en, reads, writes, extra=()):
        st = self._stream(en)
        need = {}

        def add(tok):
            if tok is None:
                return
            s, v = tok
            if need.get(s, 0) < v:
                need[s] = v
        for b in reads:
            add(b.w)
        for b in writes:
            add(b.w)
            for s, v in b.r.items():
                add((s, v))
        for t in extra:
            add(t)
        out = []
        for s, v in need.items():
            if en == "pe" and s == self.eng["pe"].sem:
                continue
            if st.seen.get(s, 0) >= v:
                continue
            st.seen[s] = v
            out.append((s, v))
        return out

    def _commit(self, tok, reads, writes):
        s, v = tok
        for b in reads:
            if b.r.get(s, 0) < v:
                b.r[s] = v
        for b in writes:
            b.w = tok
            b.r = {}

    def op(self, en, fn, R=(), W=(), inc=True):
        e = self.eng[en]
        waits = self._waits(en, R, W)
        self.n_ops += 1
        if inc:
            e.count += 1
            tok = (e.sem, e.count)
            e.ops.append((waits, fn, (e.sem, 1)))
            self._commit(tok, R, W)
            for (r, w) in e.pending:
                self._commit(tok, r, w)
            e.pending = []
        else:
            e.ops.append((waits, fn, None))
            e.pending.append((tuple(R), tuple(W)))

    def dma(self, q, out, in_, R=(), W=(), slow=False):
        e = self.eng[q]
        i = e.rr % NQD
        e.rr += 1
        extra = []
        if e.qcnt[i] > 0:
            extra.append((e.qsems[i], 16 * e.qcnt[i]))
        waits = self._waits(q, R, W, extra)
        e.qcnt[i] += 1
        tok = (e.qsems[i], 16 * e.qcnt[i])
        st = self._stream(q)
        if slow:
            st.ops.append((waits, (lambda eng, o=out, s=in_: eng.dma_start(out=o, in_=s, allow_slow_non_contiguous=True)),
                           (e.qsems[i], 16)))
        else:
            st.ops.append((waits, (lambda eng, o=out, s=in_: eng.dma_start(out=o, in_=s)), (e.qsems[i], 16)))
        self._commit(tok, R, W)
        self.n_ops += 1

    def barrier(self):
        toks = []
        for n in ("pe", "act", "dve", "pool"):
            e = self.eng[n]
            assert not e.pending
            if e.count:
                toks.append((e.sem, e.count))
        for n in ("sp", "pq"):
            e = self.eng[n]
            for i in range(NQD):
                if e.qcnt[i]:
                    toks.append((e.qsems[i], 16 * e.qcnt[i]))
        for n in ("pe", "act", "dve", "pool", "sp"):
            st = self.eng[n]
            ws = []
            for s, v in toks:
                if st.seen.get(s, 0) >= v:
                    continue
                if n != "sp" and s == st.sem:
                    pass
                st.seen[s] = v
                ws.append((s, v))
            if ws:
                st.ops.append((ws, None, None))

    def replay(self, en, e):
        for waits, fn, inc in self.eng[en].ops:
            for s, v in waits:
                e.wait_ge(self.sems[s], v)
            if fn is None:
                continue
            ins = fn(e)
            if inc is not None:
                ins.then_inc(self.sems[inc[0]], inc[1])

    def mm(self, out, lhsT, rhs, start, stop, R, W, inc=None):
        if inc is None:
            inc = stop
        self.op("pe", lambda e: e.matmul(out, lhsT=lhsT, rhs=rhs, start=start, stop=stop), R, W, inc)

    def tr(self, out, in_, ident, R, W, inc=True):
        self.op("pe", lambda e: e.transpose(out, in_, ident), R, W, inc)

    def act(self, out, in_, func, R, W, bias=None, scale=None, accum=None, en="act"):
        kw = {}
        if bias is not None:
            kw["bias"] = bias
        if scale is not None:
            kw["scale"] = scale
        if accum is not None:
            kw["accum_out"] = accum
        self.op(en, lambda e: e.activation(out=out, in_=in_, func=func, **kw), R, W)

    def tt(self, en, out, in0, in1, op, R, W):
        self.op(en, lambda e: e.tensor_tensor(out=out, in0=in0, in1=in1, op=op), R, W)

    def ts(self, en, out, in0, s1, s2, op0, op1, R, W):
        if s2 is None:
            self.op(en, lambda e: e.tensor_scalar(out=out, in0=in0, scalar1=s1, scalar2=None, op0=op0), R, W)
        else:
            self.op(en, lambda e: e.tensor_scalar(out=out, in0=in0, scalar1=s1, scalar2=s2, op0=op0, op1=op1), R, W)

    def stt(self, en, out, in0, scalar, in1, op0, op1, R, W):
        self.op(en, lambda e: e.scalar_tensor_tensor(out=out, in0=in0, scalar=scalar, in1=in1, op0=op0, op1=op1), R, W)

    def copy(self, en, out, in_, R, W):
        if en == "act":
            self.op(en, lambda e: e.activation(out=out, in_=in_, func=AF.Copy), R, W)
        else:
            self.op(en, lambda e: e.tensor_copy(out=out, in_=in_), R, W)

    def recip(self, out, in_, R, W):
        self.op("dve", lambda e: e.reciprocal(out=out, in_=in_), R, W)

    def scan(self, out, d0, d1, init, R, W):
        self.op("dve", lambda e: e.tensor_tensor_scan(out=out, data0=d0, data1=d1, initial=init,
                                                        op0=ALU.mult, op1=ALU.add), R, W)

    def memset(self, en, ap, val, W):
        self.op(en, lambda e: e.memset(ap, val), (), W)


class Arena:
    def __init__(self, nc, nbytes):
        self.t = nc.alloc_sbuf_tensor("arena", [128, nbytes // 4], F32).ap()
        self.top = 0
        self.cap = nbytes

    def f32(self, n, parts=128):
        off = self.top
        self.top += ((n * 4 + 31) // 32) * 32
        assert self.top <= self.cap, f"SBUF overflow {self.top}"
        return self.t[0:parts, off // 4: off // 4 + n]

    def bf16(self, n, parts=128):
        assert n % 2 == 0
        off = self.top
        self.top += ((n * 2 + 31) // 32) * 32
        assert self.top <= self.cap, f"SBUF overflow {self.top}"
        return self.t[0:parts, off // 4: off // 4 + n // 2].bitcast(BF16)


class Pool:
    def __init__(self, tiles, track=False):
        self.tiles = tiles
        self.bufs = [Buf() for _ in tiles]
        self.i = 0
        self.track = track
        self.held = [False] * len(tiles)

    def get(self):
        n = len(self.tiles)
        for _ in range(n):
            k = self.i % n
            self.i += 1
            if not self.held[k]:
                break
        else:
            raise RuntimeError("pool exhausted")
        if self.track:
            self.held[k] = True
        return self.tiles[k], self.bufs[k]

    def rel(self, buf):
        for k, b in enumerate(self.bufs):
            if b is buf:
                self.held[k] = False
                return
        raise KeyError("buf not in pool")


def build(cfg):
    nc = bass.Bass("TRN2", target_bir_lowering=False)
    P = Prog(nc)
    jobs = cfg["jobs"]
    dbg = cfg.get("debug", False)
    stages = cfg.get("stages", "01234")

    def din(name, shape, dt=F32):
        return nc.dram_tensor(name, list(shape), dt, kind="ExternalInput").ap()

    def dscr(name, shape, dt):
        kind = "ExternalOutput" if dbg else "Internal"
        return nc.dram_tensor(name, list(shape), dt, kind=kind).ap()

    w_in = din("w_in", [D, 2880])
    w_pesw = din("w_pesw", [D, 64])
    w_uq = din("w_uq", [512, 1536])
    w_uqsw = din("w_uqsw", [512, 512])
    w_ukv = din("w_ukv", [256, 2048])
    w_rg = din("w_rg", [2, 2, 8, 128, 128])
    w_out = din("w_out", [D, D])
    w_gate = din("w_gate", [D, DFF])
    w_up = din("w_up", [D, DFF])
    w_down = din("w_down", [DFF, D])
    ident_d = din("ident", [128, 128])
    params_d = din("params", [128, NPAR])
    msk_d = din("msk", [128, 8])

    J = {}
    for (jn, S, NOWN) in jobs:
        j = {"S": S, "NOWN": NOWN}
        j["x"] = din("x" + jn, [S, D])
        j["rope"] = din("rope" + jn, [2, 64, S])
        j["y"] = nc.dram_tensor("y" + jn, [NOWN, D], F32, kind="ExternalOutput").ap()
        j["Kn"] = dscr("Kn" + jn, [NH, 128, S], BF16)
        j["Kp"] = dscr("Kp" + jn, [NH, 64, S], BF16)
        j["V"] = dscr("V" + jn, [NH, 128, S // 128, 128], BF16)
        j["Qn"] = dscr("Qn" + jn, [NH, 128, NOWN], BF16)
        j["Qp"] = dscr("Qp" + jn, [NH, 64, NOWN], BF16)
        j["U"] = dscr("U" + jn, [8, 128, S + 8], F32)
        j["G"] = dscr("G" + jn, [8, 128, NOWN], F32)
        j["Hf"] = dscr("Hf" + jn, [8, 128, NOWN], F32)
        j["MIX"] = dscr("MIX" + jn, [16, 128, NOWN], F32)
        J[jn] = j
    NSL_IN = 6
    NSL = NSL_IN + 4 + 22 + 16
    wslab = nc.dram_tensor("wslab", [NSL, 128, 8192], BF16, kind="Internal").ap()
    SL_OUT = NSL_IN
    SL_GU = SL_OUT + 4
    SL_DN = SL_GU + 22

    A = Arena(nc, 212480)
    PS = nc.alloc_psum_tensor("ps", [128, 4096], F32).ap()
    bank = [PS[:, b * 512:(b + 1) * 512] for b in range(8)]
    bankb = [Buf(f"bank{b}") for b in range(8)]
    PSB = PS[:, 0:1024].bitcast(BF16)

    ident = A.bf16(128)
    ones = A.bf16(128)
    par = A.f32(NPAR)
    msk = A.f32(8)
    clam = A.f32(16)
    gqs = A.f32(3)
    epst = A.f32(1)
    one_t = A.f32(1)
    cB = Buf("consts")
    slabB = [Buf(f"slab{i}") for i in range(NSL)]
    wrg_c = A.bf16(2 * 2 * 8 * 128).rearrange("p (g d n o) -> p g d n o", g=2, d=2, n=8)
    wB2_c = Buf("wrg")
    carry_c = A.f32(16)
    for g_ in range(2):
        for d__ in range(2):
            P.dma("pq", wrg_c[:, g_, d__], w_rg[g_, d__].rearrange("n c o -> c n o"), (), (wB2_c,))

    P.dma("pq", ident, ident_d, (), (cB,))
    P.dma("sp", par, params_d, (), (cB,))
    P.dma("sp", msk, msk_d, (), (cB,))
    P.memset("dve", ones, 1.0, (cB,))
    ones32 = A.f32(128)
    P.memset("dve", ones32, 1.0, (cB,))
    P.memset("dve", epst, EPS, (cB,))
    P.memset("dve", one_t, 1.0, (cB,))
    P.act(clam, par[:, PC["lam"]:PC["lam"] + 16], AF.Exp, (cB,), (cB,), scale=-1.0)
    P.act(clam, clam, AF.Ln, (cB,), (cB,), bias=one_t[:, 0:1])
    P.ts("dve", clam, clam, -8.0, None, ALU.mult, None, (cB,), (cB,))
    P.ts("dve", gqs, par[:, PC["g_qn"]:PC["g_qn"] + 3], float(192.0 ** -0.5), None, ALU.mult, None, (cB,), (cB,))

    def pcol(name, i=0, n=1, parts=128):
        return par[0:parts, PC[name] + i: PC[name] + i + n]

    def slab_view(i, ncol=512, nrow=16):
        return wslab[i].rearrange("p (r c) -> p r c", c=512)[:, 0:nrow, 0:ncol]

    if "0" in stages:
        win_v = w_in.rearrange("(dc p) c -> p dc c", p=128)
        pesw_v = w_pesw.rearrange("(dc p) c -> p dc c", p=128)
        sv = wslab[0].rearrange("p (r c) -> p r c", c=512)
        P.dma("pq", sv[:, :, 0:320], win_v[:, :, 512:832], (), (slabB[0],))
        P.dma("pq", sv[:, :, 320:384], pesw_v, (), (slabB[0],))
        P.dma("pq", slab_view(1), win_v[:, :, 832:1344], (), (slabB[1],))
        P.dma("pq", slab_view(2), win_v[:, :, 1344:1856], (), (slabB[2],))
        P.dma("pq", slab_view(3), win_v[:, :, 0:512], (), (slabB[3],))
        P.dma("pq", slab_view(4), win_v[:, :, 1856:2368], (), (slabB[4],))
        P.dma("pq", slab_view(5), win_v[:, :, 2368:2880], (), (slabB[5],))

    s0b_done = []

    def emit_s0b():
        if s0b_done or "0" not in stages:
            return
        s0b_done.append(1)
        wo_v = w_out.rearrange("(g p) c -> p g c", p=128)
        for cg in range(4):
            P.dma("pq", slab_view(SL_OUT + cg), wo_v[:, :, cg * 512:(cg + 1) * 512], (), (slabB[SL_OUT + cg],))
        wg_v = w_gate.rearrange("(dc p) f -> p dc f", p=128)
        wu_v = w_up.rearrange("(dc p) f -> p dc f", p=128)
        for s in range(11):
            P.dma("pq", slab_view(SL_GU + 2 * s), wg_v[:, :, s * 512:(s + 1) * 512], (), (slabB[SL_GU + 2 * s],))
            P.dma("pq", slab_view(SL_GU + 2 * s + 1), wu_v[:, :, s * 512:(s + 1) * 512], (), (slabB[SL_GU + 2 * s + 1],))
        wd_v = w_down.rearrange("(f p) c -> p f c", p=128)
        for cg in range(4):
            for fg in range(4):
                i = SL_DN + cg * 4 + fg
                P.dma("pq", slab_view(i, 512, 11), wd_v[:, fg * 11:(fg + 1) * 11, cg * 512:(cg + 1) * 512], (), (slabB[i],))

    mark0 = A.top

    def rstd_from_bank(bk, bkB, scale, out_tile, outB, parts=128):
        P.act(out_tile[0:parts], bk[0:parts], AF.Ln, (bkB, cB), (outB,), bias=epst[0:parts, 0:1], scale=scale)
        P.act(out_tile[0:parts], out_tile[0:parts], AF.Exp, (outB,), (outB,), scale=-0.5)

    class SlabStream:
        def __init__(self, ring, seq):
            self.ring = ring
            self.seq = list(seq)
            self.issued = 0
            self.tiles = {}

        def ensure(self, upto):
            while self.issued < min(upto + 1, len(self.seq)):
                i = self.seq[self.issued]
                t, b = self.ring.get()
                t3 = t.rearrange("p (r c) -> p r c", c=512)
                w3 = wslab[i].rearrange("p (r c) -> p r c", c=512)
                if i == 0:
                    P.dma("sp", t3[:, :, 0:384], w3[:, :, 0:384], (slabB[i],), (b,))
                elif i >= SL_DN:
                    P.dma("sp", t3[:, 0:11, :], w3[:, 0:11, :], (slabB[i],), (b,))
                else:
                    P.dma("sp", t, wslab[i], (slabB[i],), (b,))
                self.tiles[self.issued] = (t3, b)
                self.issued += 1

        def get(self, n, ahead=1):
            self.ensure(n + ahead)
            return self.tiles.pop(n)

    def prep_hT(src_ap, srcB, gname, hT, hTB_t, t, xn, xnB, ssq, ssqB, rr):
        P.act(xn, src_ap, AF.Square, (srcB,), (xnB, ssqB), accum=ssq[:, 0:1])
        P.act(ssq[:, 1:2], ssq[:, 0:1], AF.Ln, (ssqB, cB), (ssqB,), bias=epst[:, 0:1], scale=1.0 / D)
        P.act(ssq[:, 2:3], ssq[:, 1:2], AF.Exp, (ssqB,), (ssqB,), scale=-0.5)
        P.ts("dve", xn, src_ap, ssq[:, 2:3], None, ALU.mult, None, (srcB, ssqB), (xnB,))
        tpB = bankb[0]
        for dc in range(16):
            P.tr(PSB[:, dc * 128:(dc + 1) * 128], xn[:, dc * 128:(dc + 1) * 128], ident, (xnB, cB),
                 (bankb[0], bankb[1]), inc=(dc == 15))
        g3 = par[:, PC[gname]:PC[gname] + 16].unsqueeze(2).to_broadcast([128, 16, 128])
        P.tt("dve", hT[:, :, t * 128:(t + 1) * 128], PSB.rearrange("p (a b) -> p a b", b=128), g3, ALU.mult,
             (bankb[0], bankb[1], cB), (hTB_t,))

    def job_gen(jn, S, NOWN):
        j = J[jn]
        NCK = S // CH
        NOC = NOWN // CH
        NSEG = S // NOWN
        SEGW = NOWN // CH
        mcol0 = 0 if NSEG == 4 else 4
        x_d, rope_d = j["x"], j["rope"]

        UBk = [[Buf(f"U{k_}_{b_}") for b_ in range(8)] for k_ in range(S // CH)]
        UhL = [Buf(f"UhL{b_}") for b_ in range(8)]
        UhR = [Buf(f"UhR{b_}") for b_ in range(8)]
        LR = {}

        def lru_setup(n_u, n_tp, banks):
            if "wrg" not in LR:
                LR["wrg"] = wrg_c
                LR["wB2"] = wB2_c
                LR["carry"] = carry_c
                LR["carB"] = [[Buf(f"carry{d_}_{b_}") for b_ in range(8)] for d_ in range(2)]
                P.memset("dve", carry_c, 0.0, [bb_ for l_ in LR["carB"] for bb_ in l_])
            LR["ub"] = Pool([A.f32(CH + 3) for _ in range(n_u)], track=True)
            LR["cpp"] = Pool([A.f32(CH) for _ in range(n_u)], track=True)
            LR["xcbp"] = Pool([A.bf16(CH) for _ in range(n_u)], track=True)
            LR["tp"] = Pool([A.f32(CH) for _ in range(n_tp)], track=True)
            LR["lpp"] = Pool([bank[b] for b in banks])
            LR["lpp"].bufs = [bankb[b] for b in banks]

        NCK = S // CH
        NOC = NOWN // CH
        NSEG = S // NOWN
        SEGW = NOWN // CH
        mcol0 = 0 if NSEG == 4 else 4
        HfB = [Buf(f"Hf{w}") for w in range(NOC)]
        MIXB = j["MIXB"] = [[Buf(f"MIX{g}_{w}") for w in range(NOC)] for g in range(16)]
        U = j["U"]

        def lru_unit_outer(w, d_, blks):
            ub, cpp, xcbp, tp, lpp = LR["ub"], LR["cpp"], LR["xcbp"], LR["tp"], LR["lpp"]
            wrg, wB2, carry, carB = LR["wrg"], LR["wB2"], LR["carry"], LR["carB"]
            def mcol(jb):
                return msk[:, mcol0 + jb - 1: mcol0 + jb]

            def lru_unit(w, d_, blks):
                own = w < NOC
                us, xcs, xcbs, gates, rs, is_, as_ = {}, {}, {}, {}, {}, {}, {}
                held = []
                for blk in blks:
                    u, uB = ub.get(); held.append((ub, uB))
                    deps_ = [UBk[w][blk], UBk[w - 1][blk] if w > 0 else UhL[blk],
                             UBk[w + 1][blk] if w + 1 < NCK else UhR[blk]]
                    P.dma("sp", u, U[blk, :, CH * w: CH * w + CH + 3], deps_, (uB,))
                    if (w % SEGW) == 0:
                        jb = w // SEGW if w > 0 else NSEG
                        P.ts("dve", u[:, 0:1], u[:, 0:1], mcol(jb), None, ALU.mult, None, (uB, cB), (uB,))
                    if ((w + 1) % SEGW) == 0:
                        jb = (w + 1) // SEGW
                        P.ts("dve", u[:, CH + 1:CH + 3], u[:, CH + 1:CH + 3], mcol(jb), None, ALU.mult, None,
                             (uB, cB), (uB,))
                    us[blk] = (u, uB)
                yield
                for blk in blks:
                    u, uB = us[blk]
                    cw = PC["conv_w"] + blk * 4
                    xc, xcB = cpp.get(); held.append((cpp, xcB))
                    P.act(xc, u[:, 0:CH], AF.Identity, (uB, cB), (xcB,), bias=pcol("conv_b", blk),
                          scale=par[:, cw:cw + 1])
                    xcs[blk] = (xc, xcB)
                yield
                for blk in blks:
                    u, uB = us[blk]
                    cw = PC["conv_w"] + blk * 4
                    xc, xcB = xcs[blk]
                    for tap in range(1, 4):
                        P.stt("dve", xc, u[:, tap:tap + CH], par[:, cw + tap:cw + tap + 1], xc, ALU.mult, ALU.add,
                              (uB, cB, xcB), (xcB,))
                yield
                for blk in blks:
                    xc, xcB = xcs[blk]
                    xcb, xcbB = xcbp.get(); held.append((xcbp, xcbB))
                    P.copy("act", xcb, xc, (xcB,), (xcbB,))
                    xcbs[blk] = (xcb, xcbB)
                yield
                for blk in blks:
                    xcb, xcbB = xcbs[blk]
                    rhs = xcb if d_ == 0 else xcb[:, ::-1]
                    br, brB = lpp.get()
                    P.mm(br, wrg[:, 0, d_, blk, :], rhs, True, True, (wB2, xcbB), (brB,))
                    bi, biB = lpp.get()
                    P.mm(bi, wrg[:, 1, d_, blk, :], rhs, True, True, (wB2, xcbB), (biB,))
                    r_, rB = tp.get(); held.append((tp, rB))
                    P.act(r_, br, AF.Sigmoid, (brB, cB), (rB,), bias=pcol("b_r", d_ * 8 + blk))
                    rs[blk] = (r_, rB)
                    i_, iB = tp.get(); held.append((tp, iB))
                    P.act(i_, bi, AF.Sigmoid, (biB, cB), (iB,), bias=pcol("b_i", d_ * 8 + blk))
                    is_[blk] = (i_, iB)
                yield
                for blk in blks:
                    r_, rB = rs[blk]
                    a_, aB = tp.get(); held.append((tp, aB))
                    P.act(a_, r_, AF.Exp, (rB, cB), (aB,), scale=clam[:, d_ * 8 + blk: d_ * 8 + blk + 1])
                    as_[blk] = (a_, aB)
                yield
                for blk in blks:
                    r_, rB = rs[blk]
                    a_, aB = as_[blk]
                    P.tt("pool", r_, a_, a_, ALU.mult, (aB,), (rB,))
                yield
                for blk in blks:
                    r_, rB = rs[blk]
                    P.act(r_, r_, AF.Sqrt, (rB, cB), (rB,), bias=one_t[:, 0:1], scale=-1.0)
                yield
                for blk in blks:
                    xc, xcB = xcs[blk]
                    xcv = xc if d_ == 0 else xc[:, ::-1]
                    r_, rB = rs[blk]
                    i_, iB = is_[blk]
                    P.tt("pool", i_, i_, xcv, ALU.mult, (iB, xcB), (iB,))
                    P.tt("pool", i_, i_, r_, ALU.mult, (iB, rB), (iB,))
                yield
                hs = {}
                for blk in blks:
                    i_, iB = is_[blk]
                    a_, aB = as_[blk]
                    cc = carry[:, d_ * 8 + blk: d_ * 8 + blk + 1]
                    ccB = carB[d_][blk]
                    if d_ == 0 and (w % SEGW) == 0:
                        jb = w // SEGW if w > 0 else NSEG
                        P.ts("dve", cc, cc, mcol(jb), None, ALU.mult, None, (ccB, cB), (ccB,))
                    if d_ == 1 and ((w + 1) % SEGW) == 0:
                        jb = (w + 1) // SEGW
                        P.ts("dve", cc, cc, mcol(jb), None, ALU.mult, None, (ccB, cB), (ccB,))
                    h_, hB = rs[blk]
                    P.scan(h_, a_, i_, cc, (aB, iB, ccB), (hB,))
                    P.copy("dve", cc, h_[:, CH - 1:CH], (hB,), (ccB,))
                    hs[blk] = (h_, hB)
                yield
                for blk in blks:
                    h_, hB = hs[blk]
                    if own and d_ == 0:
                        P.dma("sp", j["Hf"][blk, :, CH * w: CH * w + CH], h_, (hB,), (HfB[w],))
                    if own and d_ == 1:
                        hf, hfB = tp.get(); held.append((tp, hfB))
                        P.dma("sp", hf, j["Hf"][blk, :, CH * w: CH * w + CH], (HfB[w],), (hfB,))
                        gt, gtB = tp.get(); held.append((tp, gtB))
                        P.dma("sp", gt, j["G"][blk, :, CH * w: CH * w + CH], (j["GB"][w],) if "GB" in j else (), (gtB,))
                        P.tt("pool", hf, hf, h_[:, ::-1], ALU.add, (hfB, hB), (hfB,))
                        P.tt("pool", hf, hf, gt, ALU.mult, (hfB, gtB), (hfB,))
                        P.dma("sp", j["MIX"][8 + blk, :, CH * w: CH * w + CH], hf, (hfB,), (MIXB[8 + blk][w],))
                for pl_, b_ in held:
                    pl_.rel(b_)


            yield from lru_unit(w, d_, blks)

        def lockstep(gens, skew=0):
            active = list(enumerate(gens))
            t_ = 0
            while active:
                for item in list(active):
                    j_, g_ = item
                    if t_ < skew * j_:
                        continue
                    try:
                        next(g_)
                    except StopIteration:
                        active.remove(item)
                t_ += 1

        if "1" in stages:
            A.top = mark0
            wuq = A.bf16(4 * 1536).rearrange("p (c n) -> p c n", c=4)
            wuqs = A.bf16(4 * 512).rearrange("p (c n) -> p c n", c=4)
            wk = A.bf16(2 * 1024).rearrange("p (c h d) -> p c h d", c=2, h=8)
            wv = A.bf16(2 * 1024).rearrange("p (c h d) -> p c h d", c=2, h=8)
            wB = Buf("wres")
            P.dma("pq", wuq, w_uq.rearrange("(c p) n -> p c n", p=128), (), (wB,))
            P.dma("pq", wuqs, w_uqsw.rearrange("(c p) n -> p c n", p=128), (), (wB,))
            ukv_v = w_ukv.rearrange("(c p) (h two d) -> p c h two d", p=128, two=2, d=128)
            for c in range(2):
                P.dma("pq", wk[:, c], ukv_v[:, c, :, 0, :], (), (wB,))
                P.dma("pq", wv[:, c], ukv_v[:, c, :, 1, :], (), (wB,))
            ring = Pool([A.bf16(8192) for _ in range(2)])
            xt = Pool([A.f32(D) for _ in range(3)])
            xn4 = A.bf16(4 * D).rearrange("p (t d) -> p t d", t=4)
            xn4B = [Buf(f"xn{t}") for t in range(4)]
            ssqp = Pool([A.f32(4) for _ in range(4)])
            hT2 = [A.bf16(16 * CH).rearrange("p (a b) -> p a b", a=16) for _ in range(2)]
            hTB2 = [[Buf(f"hT{p_}_{t}") for t in range(4)] for p_ in range(2)]
            cf = A.f32(4 * CH).rearrange("p (a b) -> p a b", a=4)
            cfB = Buf("cf")
            cqn2 = [A.bf16(4 * CH).rearrange("p (a b) -> p a b", a=4) for _ in range(2)]
            cqnB2 = [Buf("cqn0"), Buf("cqn1")]
            ckvn2 = [A.bf16(2 * CH).rearrange("p (a b) -> p a b", a=2) for _ in range(2)]
            ckvnB2 = [Buf("ckvn0"), Buf("ckvn1")]
            kpf = A.f32(CH)
            kpsf = A.f32(CH)
            kptB = Buf("kptmp")
            kpr2 = [A.f32(CH) for _ in range(2)]
            sqkp2 = [A.bf16(CH) for _ in range(2)]
            kpB2 = [Buf("kp0"), Buf("kp1")]
            cst2 = [A.f32(2 * CH).rearrange("p (a b) -> p a b", a=2) for _ in range(2)]
            cstB2 = [Buf("cs0"), Buf("cs1")]
            vbuf = A.bf16(8 * 4 * 128).rearrange("p (h t d) -> p h t d", h=8, t=4)
            vB = Buf("vbuf")
            sqp_m = Pool([A.bf16(CH) for _ in range(4)])
            f32p_m = Pool([A.f32(CH) for _ in range(3)])
            sqp_h = Pool([A.bf16(CH) for _ in range(3)])
            f32p_h = Pool([A.f32(CH) for _ in range(3)])
            b16p_h = Pool([A.bf16(CH) for _ in range(4)])
            mp = Pool([bank[2], bank[3], bank[0], bank[1]])
            mp.bufs = [bankb[2], bankb[3], bankb[0], bankb[1]]
            hp = Pool([bank[b] for b in range(4, 8)])
            hp.bufs = [bankb[b] for b in range(4, 8)]

            order = list(range(NCK))
            seq = []
            for k in order:
                seq += [0, 1, 2] + ([3, 4, 5] if k < NOC else [])
            SS = SlabStream(ring, seq)
            st_ = {"sidx": 0}
            side = []

            def tick(n=1):
                for _ in range(n):
                    for g_ in list(side):
                        try:
                            next(g_)
                        except StopIteration:
                            side.remove(g_)

            def next_slab():
                r_ = SS.get(st_["sidx"])
                st_["sidx"] += 1
                return r_

            def proj16(slab, col0, m, hT, hTB, slB):
                tick(3)
                bk, bb = mp.get()
                for dc in range(16):
                    P.mm(bk[0:m], slab[:, dc, col0:col0 + m], hT[:, dc, :], dc == 0, dc == 15, (slB, *hTB), (bb,))
                return bk, bb

            xt_cur = {}

            def xprep_load(k, ts=(0, 1, 2)):
                c0 = k * CH
                for t in ts:
                    xt_t, xtB = xt.get()
                    P.dma("pq", xt_t, x_d[c0 + t * 128: c0 + (t + 1) * 128, :], (), (xtB,))
                    xt_cur[(k, t)] = (xt_t, xtB)

            def xprep_elem(k):
                for t in range(4):
                    if t == 1:
                        xprep_load(k, (3,))
                    xt_t, xtB = xt_cur.pop((k, t))
                    ssq, ssqB = ssqp.get()
                    xn = xn4[:, t, :]
                    P.act(xn, xt_t, AF.Square, (xtB,), (xn4B[t], ssqB), accum=ssq[:, 0:1])
                    P.act(ssq[:, 1:2], ssq[:, 0:1], AF.Ln, (ssqB, cB), (ssqB,), bias=epst[:, 0:1], scale=1.0 / D)
                    P.act(ssq[:, 2:3], ssq[:, 1:2], AF.Exp, (ssqB,), (ssqB,), scale=-0.5)
                    P.ts("dve", xn, xt_t, ssq[:, 2:3], None, ALU.mult, None, (xtB, ssqB), (xn4B[t],))

            def xprep_tr(par_):
                hT, hTB = hT2[par_], hTB2[par_]
                g3 = par[:, PC["g_attn"]:PC["g_attn"] + 16].unsqueeze(2).to_broadcast([128, 16, 128])
                for t in range(4):
                    for dc in range(16):
                        P.tr(PSB[:, dc * 128:(dc + 1) * 128], xn4[:, t, dc * 128:(dc + 1) * 128], ident, (xn4B[t], cB),
                             (bankb[0], bankb[1]), inc=(dc == 15))
                    P.tt("dve", hT[:, :, t * 128:(t + 1) * 128], PSB.rearrange("p (a b) -> p a b", b=128), g3, ALU.mult,
                         (bankb[0], bankb[1], cB), (hTB[t],))

            def nf_part1(ngrp, slab, slB, hT, hTB):
                sqs = []
                for g in range(ngrp):
                    bk, bb = proj16(slab, g * 128, 128, hT, hTB, slB)
                    P.copy("act", cf[:, g, :], bk, (bb,), (cfB,))
                    sq, sqB = sqp_m.get()
                    P.act(sq, bk, AF.Square, (bb,), (sqB,))
                    sqs.append((sq, sqB))
                return sqs

            def nf_part2(sqs, gname, width, outn, outB):
                ngrp = len(sqs)
                sb, sbB = mp.get()
                for g, (sq, sqB) in enumerate(sqs):
                    P.mm(sb, ones, sq, g == 0, g == ngrp - 1, (cB, sqB), (sbB,))
                rt, rtB = f32p_m.get()
                rstd_from_bank(sb, sbB, 1.0 / width, rt, rtB)
                for g in range(ngrp):
                    P.stt("dve", outn[:, g, :], cf[:, g, :], pcol(gname, g), rt, ALU.mult, ALU.mult,
                          (cfB, rtB, cB), (outB,))

            def qk_head(wn_fn, wp_fn, wps_fn, nc_, src, srcB, shared, cst, cstB, gn, gp, gps, dst_n, dst_p, c0):
                bn, bnB = hp.get()
                for c in range(nc_):
                    P.mm(bn, wn_fn(c), src[:, c, :], c == 0, c == nc_ - 1, (wB, srcB), (bnB,))
                sq, sqB = sqp_h.get()
                P.act(sq, bn, AF.Square, (bnB,), (sqB,))
                if shared is None:
                    bp, bpB = hp.get()
                    for c in range(nc_):
                        P.mm(bp[0:64], wp_fn(c), src[:, c, :], c == 0, c == nc_ - 1, (wB, srcB), (bpB,))
                    bs, bsB = hp.get()
                    for c in range(nc_):
                        P.mm(bs[0:64], wps_fn(c), src[:, c, :], c == 0, c == nc_ - 1, (wB, srcB), (bsB,))
                    sq2, sq2B = sqp_h.get()
                    P.act(sq2[0:64], bp[0:64], AF.Square, (bpB,), (sq2B,))
                else:
                    kpr, sq2, sq2B = shared
                yield
                sb, sbB = hp.get()
                P.mm(sb, ones, sq, True, False, (cB, sqB), (sbB,))
                P.mm(sb, ones[0:64, :], sq2[0:64], False, True, (cB, sq2B), (sbB,))
                rt, rtB = f32p_h.get()
                rstd_from_bank(sb, sbB, 1.0 / 192.0, rt, rtB)
                yield
                on, onB = b16p_h.get()
                P.stt("dve", on, bn, gn, rt, ALU.mult, ALU.mult, (bnB, rtB, cB), (onB,))
                P.dma("sp", dst_n[:, c0:c0 + CH], on, (onB,), ())
                op_, opB = b16p_h.get()
                if shared is None:
                    t1, t1B = f32p_h.get()
                    t2, t2B = f32p_h.get()
                    P.stt("dve", t1[0:64], bp[0:64], gp, cst[0:64, 0, :], ALU.mult, ALU.mult, (bpB, cstB, cB), (t1B,))
                    P.stt("dve", t2[0:64], bs[0:64], gps, cst[0:64, 1, :], ALU.mult, ALU.mult, (bsB, cstB, cB), (t2B,))
                    P.tt("dve", t1[0:64], t1[0:64], t2[0:64], ALU.add, (t1B, t2B), (t1B,))
                    P.tt("dve", op_[0:64], t1[0:64], rt[0:64], ALU.mult, (t1B, rtB), (opB,))
                else:
                    P.tt("dve", op_[0:64], kpr[0:64], rt[0:64], ALU.mult, (sq2B, rtB), (opB,))
                P.dma("sp", dst_p[:, c0:c0 + CH], op_[0:64], (opB,), ())
                yield

            def heads(k, par_):
                own = k < NOC
                c0 = k * CH
                ckvn, ckvnB = ckvn2[par_], ckvnB2[par_]
                cst, cstB = cst2[par_], cstB2[par_]
                for h in range(NH):
                    yield from qk_head(lambda c, h=h: wk[:, c, h, :], None, None, 2, ckvn, ckvnB,
                                       (kpr2[par_], sqkp2[par_], kpB2[par_]), cst, cstB,
                                       pcol("g_kn"), None, None, j["Kn"][h], j["Kp"][h], c0)
                for t in range(4):
                    for jj in range(2):
                        bk, bb = hp.get()
                        for c in range(2):
                            P.mm(bk, ckvn[:, c, t * 128:(t + 1) * 128], wv[:, c, 4 * jj:4 * jj + 4, :],
                                 c == 0, c == 1, (ckvnB, wB), (bb,))
                        P.copy("act" if jj == 0 else "dve", vbuf[:, 4 * jj:4 * jj + 4, t, :],
                               bk.rearrange("p (h d) -> p h d", h=4), (bb,), (vB,))
                    yield
                for h in range(NH):
                    P.dma("sp", j["V"][h, :, 4 * k:4 * k + 4, :], vbuf[:, h], (vB,), ())
                yield
                if own:
                    cqn, cqnB = cqn2[par_], cqnB2[par_]
                    for h in range(NH):
                        yield from qk_head(lambda c, h=h: wuq[:, c, h * 192:h * 192 + 128],
                                           lambda c, h=h: wuq[:, c, h * 192 + 128:h * 192 + 192],
                                           lambda c, h=h: wuqs[:, c, h * 64:(h + 1) * 64], 4, cqn, cqnB, None, cst, cstB,
                                           gqs[:, 0:1], gqs[0:64, 1:2], gqs[0:64, 2:3], j["Qn"][h], j["Qp"][h], c0)

            GB = j["GB"] = [Buf(f"G{k}") for k in range(NOC)]

            xprep_load(order[0])
            xprep_elem(order[0])
            xprep_tr(0)
            if len(order) > 1:
                xprep_load(order[1])
            emit_s0b()
            for i, k in enumerate(order):
                par_ = i % 2
                own = k < NOC
                c0 = k * CH
                hT, hTB = hT2[par_], hTB2[par_]
                cst, cstB = cst2[par_], cstB2[par_]
                kpr, sqkp, kpB = kpr2[par_], sqkp2[par_], kpB2[par_]
                if i + 1 < len(order) and i > 0:
                    xprep_load(order[i + 1])
                P.dma("sp", cst[0:64], rope_d[:, :, c0:c0 + CH].rearrange("a p n -> p a n"), (), (cstB,))
                sl, slB = next_slab()
                sq_ckv = nf_part1(2, sl, slB, hT, hTB)
                bk, bb = proj16(sl, 256, 64, hT, hTB, slB)
                P.copy("act", kpf[0:64], bk[0:64], (bb,), (kptB,))
                P.act(sqkp[0:64], bk[0:64], AF.Square, (bb,), (kpB,))
                bk, bb = proj16(sl, 320, 64, hT, hTB, slB)
                P.copy("act", kpsf[0:64], bk[0:64], (bb,), (kptB,))
                P.stt("dve", kpr[0:64], kpf[0:64], pcol("g_kp", 0, 1, 64), cst[0:64, 0, :], ALU.mult, ALU.mult,
                      (kptB, cstB, cB), (kpB,))
                P.stt("dve", kpsf[0:64], kpsf[0:64], pcol("g_kps", 0, 1, 64), cst[0:64, 1, :], ALU.mult, ALU.mult,
                      (kptB, cstB, cB), (kptB,))
                P.tt("dve", kpr[0:64], kpr[0:64], kpsf[0:64], ALU.add, (kpB, kptB), (kpB,))
                for s_ in range(2):
                    sl, slB = next_slab()
                    for g in range(4):
                        bk, bb = proj16(sl, g * 128, 128, hT, hTB, slB)
                        ft, ftB = f32p_m.get()
                        P.copy("act" if g % 2 == 0 else "dve", ft, bk, (bb,), (ftB,))
                        blk = s_ * 4 + g
                        P.dma("sp", j["U"][blk, :, 1 + c0: 1 + c0 + CH], ft, (ftB,), (UBk[k][blk],))
                        if k == 0:
                            P.dma("sp", j["U"][blk, :, 1 + S: 1 + S + 2], ft[:, 0:2], (ftB,), (UhR[blk],))
                        if k == NCK - 1:
                            P.dma("sp", j["U"][blk, :, 0:1], ft[:, CH - 1:CH], (ftB,), (UhL[blk],), slow=True)
                    if s_ == 0:
                        nf_part2(sq_ckv, "g_kva", 256.0, ckvn2[par_], ckvnB2[par_])
                        if i + 1 < len(order):
                            xprep_elem(order[i + 1])
                if own:
                    sl, slB = next_slab()
                    sq_cq = nf_part1(4, sl, slB, hT, hTB)
                    for s_ in range(2):
                        sl, slB = next_slab()
                        for g in range(4):
                            bk, bb = proj16(sl, g * 128, 128, hT, hTB, slB)
                            ft, ftB = f32p_m.get()
                            g2, g2B = f32p_m.get()
                            P.act(g2, bk, AF.Square, (bb,), (g2B,))
                            P.ts("dve", g2, g2, 0.044715, 1.0, ALU.mult, ALU.add, (g2B,), (g2B,))
                            P.tt("dve", g2, g2, bk, ALU.mult, (g2B, bb), (g2B,))
                            P.act(g2, g2, AF.Sigmoid, (g2B,), (g2B,), scale=float(2.0 * np.sqrt(2.0 / np.pi)))
                            P.tt("dve", ft, g2, bk, ALU.mult, (g2B, bb), (ftB,))
                            P.dma("sp", j["G"][s_ * 4 + g, :, c0:c0 + CH], ft, (ftB,), (GB[k],))
                        if s_ == 0:
                            nf_part2(sq_cq, "g_qa", 512.0, cqn2[par_], cqnB2[par_])
                while side:
                    tick()
                if i + 1 < len(order):
                    xprep_tr(1 - par_)
                side.append(heads(k, par_))
            while side:
                tick()
            P.barrier()
        yield

        if "2" in stages:
            A.top = mark0
            lru_setup(16, 48, list(range(8)))

            def lru_stream(ws, d_, half):
                for w in ws:
                    yield from lru_unit_outer(w, d_, [4 * half + b_ for b_ in range(4)])

            wf_, wb_ = list(range(NOC, NCK)), list(range(NCK - 1, NOC - 1, -1))
            lockstep([lru_stream(wf_, 0, 0), lru_stream(wb_, 1, 0), lru_stream(wf_, 0, 1), lru_stream(wb_, 1, 1)], skew=3)
            for w in range(NOC):
                lockstep([lru_unit_outer(w, 0, [0, 1, 2, 3]), lru_unit_outer(w, 0, [4, 5, 6, 7])], skew=5)
            for w in range(NOC - 1, -1, -1):
                lockstep([lru_unit_outer(w, 1, [0, 1, 2, 3]), lru_unit_outer(w, 1, [4, 5, 6, 7])], skew=5)
            P.barrier()
        yield

        if "3" in stages:
            A.top = mark0
            QB = 1024
            NQB = NOWN // QB
            KG = 2048 if S >= 2048 else S
            NT = KG // 128
            NKG = S // KG
            knp = Pool([A.bf16(KG) for _ in range(3)])
            kpp = Pool([A.bf16(KG) for _ in range(3)])
            vp = Pool([A.bf16(KG).rearrange("p (t d) -> p t d", d=128) for _ in range(3)])
            qnp = Pool([A.bf16(QB) for _ in range(2)])
            qpp = Pool([A.bf16(QB) for _ in range(2)])
            ptp = Pool([A.bf16(QB) for _ in range(4)])
            outp = Pool([A.f32(CH) for _ in range(4)])
            sp_ = Pool([PS[:, (2 + 2 * p) * 512:(4 + 2 * p) * 512] for p in range(3)])
            if "MIXB" not in j:
                j["MIXB"] = [[Buf() for w in range(NOC)] for g in range(16)]
            dacc = Pool([(A.f32(QB), A.f32(QB)) for _ in range(2)])
            dacc.bufs = [(Buf(), Buf()) for _ in range(2)]
            ob = [bank[0], bank[1]]
            obB = [bankb[0], bankb[1]]
            LA = 2
            for h in range(NH):
                for qb in range(NQB):
                    qn, qnB = qnp.get()
                    qp, qpB = qpp.get()
                    P.dma("sp", qn, j["Qn"][h, :, qb * QB:(qb + 1) * QB], (), (qnB,))
                    P.dma("sp", qp[0:64], j["Qp"][h, :, qb * QB:(qb + 1) * QB], (), (qpB,))
                    (accd, accq), (accdB, accqB) = dacc.get()
                    groups = {}

                    def ensure_group(kg):
                        if kg >= NKG or kg in groups:
                            return
                        kn, knB = knp.get()
                        kp, kpB_ = kpp.get()
                        vv, vvB = vp.get()
                        P.dma("sp", kn, j["Kn"][h, :, kg * KG:(kg + 1) * KG], (), (knB,))
                        P.dma("sp", kp[0:64], j["Kp"][h, :, kg * KG:(kg + 1) * KG], (), (kpB_,))
                        P.dma("sp", vv, j["V"][h, :, kg * NT:(kg + 1) * NT, :], (), (vvB,))
                        groups[kg] = (kn, knB, kp, kpB_, vv, vvB)

                    def tile(i):
                        kg, t = divmod(i, NT)
                        ensure_group(kg)
                        if t == 0:
                            ensure_group(kg + 1)
                        kn, knB, kp, kpB_, vv, vvB = groups[kg]
                        return (kn[:, t * 128:(t + 1) * 128], knB, kp[0:64, t * 128:(t + 1) * 128], kpB_,
                                vv[:, t, :], vvB)

                    n = NKG * NT
                    stq = {}

                    def emit_qk(i):
                        knt, knB, kpt, kpB_, vt, vvB = tile(i)
                        st, stB = sp_.get()
                        for hf_ in range(2):
                            P.mm(st[:, hf_ * 512:(hf_ + 1) * 512], knt, qn[:, hf_ * 512:(hf_ + 1) * 512], True, False,
                                 (knB, qnB), (stB,), inc=False)
                        for hf_ in range(2):
                            P.mm(st[:, hf_ * 512:(hf_ + 1) * 512], kpt, qp[0:64, hf_ * 512:(hf_ + 1) * 512], False, True,
                                 (kpB_, qpB), (stB,), inc=(hf_ == 1))
                        pt, ptB = ptp.get()
                        P.act(pt, st, AF.Exp, (stB,), (ptB,))
                        stq[i] = (pt, ptB, vt, vvB)

                    for i in range(min(LA, n)):
                        emit_qk(i)
                    cnt = {"dve": 0, "pool": 0}
                    for i in range(n):
                        if i + LA < n:
                            emit_qk(i + LA)
                        pt, ptB, vt, vvB = stq.pop(i)
                        for hf_ in range(2):
                            P.mm(ob[hf_], vt, pt[:, hf_ * 512:(hf_ + 1) * 512], i == 0, i == n - 1, (vvB, ptB), (obB[hf_],),
                                 inc=(hf_ == 1))
                        en_ = "pool" if i % 3 == 2 else "dve"
                        ac_, acB_ = (accd, accdB) if en_ == "dve" else (accq, accqB)
                        if cnt[en_] == 0:
                            P.copy(en_, ac_, pt, (ptB,), (acB_,))
                        else:
                            P.tt(en_, ac_, ac_, pt, ALU.add, (acB_, ptB), (acB_,))
                        cnt[en_] += 1
                    if cnt["pool"]:
                        P.tt("dve", accd, accd, accq, ALU.add, (accdB, accqB), (accdB,))
                    db, dbB = sp_.get()
                    for hf_ in range(2):
                        P.mm(db[:, hf_ * 512:(hf_ + 1) * 512], ones32, accd[:, hf_ * 512:(hf_ + 1) * 512], True, True,
                             (cB, accdB), (dbB,), inc=(hf_ == 1))
                    for hf_ in range(2):
                        rd, rdB = outp.get()
                        P.recip(rd, db[:, hf_ * 512:(hf_ + 1) * 512], (dbB,), (rdB,))
                        P.tt("dve", rd, ob[hf_], rd, ALU.mult, (obB[hf_], rdB), (rdB,))
                        c0_ = qb * QB + hf_ * 512
                        P.dma("sp", j["MIX"][h, :, c0_:c0_ + 512], rd, (rdB,), (j["MIXB"][h][c0_ // CH],))
            P.barrier()
        yield

        if "4" in stages:
            emit_s0b()
            A.top = mark0
            ring = Pool([A.bf16(8192) for _ in range(3)])
            xres = [A.f32(D) for _ in range(4)]
            xrB = [Buf(f"xres{t}") for t in range(4)]
            xn = A.bf16(D)
            xnB = Buf("xn4")
            ssq = A.f32(4)
            ssqB = Buf("ssq4")
            hT = A.bf16(16 * CH).rearrange("p (a b) -> p a b", a=16)
            hTB = [Buf(f"h2T{t}") for t in range(4)]
            mixed2 = [A.bf16(16 * CH).rearrange("p (a b) -> p a b", a=16) for _ in range(2)]
            mixedB2 = [Buf("mixed0"), Buf("mixed1")]
            big = A.bf16(NFC * CH)
            actT = big.rearrange("p (a b) -> p a b", a=NFC)
            bigB = Buf("big")
            stg = hT.rearrange("p a b -> p (a b)").bitcast(F32).rearrange("p (a b) -> p a b", a=8)
            pp23 = Pool([bank[2], bank[3]])
            pp23.bufs = [bankb[2], bankb[3]]
            sqp = Pool([A.bf16(CH) for _ in range(8)])
            f32p = Pool([A.f32(CH) for _ in range(4)])
            pp = Pool([bank[b] for b in range(2, 8)])
            pp.bufs = [bankb[b] for b in range(2, 8)]
            seq = []
            for k in range(NOC):
                seq += [SL_OUT + c for c in range(4)] + [SL_GU + s for s in range(22)] + [SL_DN + s for s in range(16)]
            SS = SlabStream(ring, seq)
            sidx = 0
            MIXB = j.get("MIXB") or [[Buf() for w in range(NOC)] for g in range(16)]
            def mix_prep(k):
                c0 = k * CH
                mixed, mixedB = mixed2[k % 2], mixedB2[k % 2]
                for half in range(2):
                    for g in range(8):
                        gg = half * 8 + g
                        P.dma("sp", stg[:, g, :], j["MIX"][gg, :, c0:c0 + CH], (MIXB[gg][k],), hTB)
                    yield
                    sqs = []
                    for g in range(8):
                        sq, sqB = sqp.get()
                        P.act(sq, stg[:, g, :], AF.Square, hTB, (sqB,))
                        sqs.append((sq, sqB))
                    yield
                    sb, sbB = pp23.get()
                    for g, (sq, sqB) in enumerate(sqs):
                        P.mm(sb, ones, sq, g == 0, g == 7, (cB, sqB), (sbB,))
                    rt, rtB = f32p.get()
                    rstd_from_bank(sb, sbB, 1.0 / 1024.0, rt, rtB)
                    yield
                    gname = "g_ao" if half == 0 else "g_lo"
                    for g in range(8):
                        gg = half * 8 + g
                        P.stt("dve", mixed[:, gg, :], stg[:, g, :], pcol(gname, g), rt, ALU.mult, ALU.mult,
                              (*hTB, rtB, cB), (mixedB,))
                    yield

            for _ in mix_prep(0):
                pass
            for k in range(NOC):
                c0 = k * CH
                mixed, mixedB = mixed2[k % 2], mixedB2[k % 2]
                for t in range(4):
                    P.dma("pq", xres[t], x_d[c0 + t * 128:c0 + (t + 1) * 128, :], (), (xrB[t],))
                for cg in range(4):
                    sl, slB = SS.get(sidx); sidx += 1
                    for t in range(4):
                        bk, bb = pp.get()
                        for g in range(16):
                            P.mm(bk, mixed[:, g, t * 128:(t + 1) * 128], sl[:, g, :], g == 0, g == 15, (mixedB, slB), (bb,))
                        xs = xres[t][:, cg * 512:(cg + 1) * 512]
                        P.tt("dve", xs, bk, xs, ALU.add, (bb, xrB[t]), (xrB[t],))
                for t in range(4):
                    prep_hT(xres[t], xrB[t], "g_ffn", hT, hTB[t], t, xn, xnB, ssq, ssqB, None)
                for s in range(11):
                    sg, sgB = SS.get(sidx); sidx += 1
                    su, suB = SS.get(sidx); sidx += 1
                    for fc in range(4):
                        bg, bgB = pp.get()
                        for dc in range(16):
                            P.mm(bg, sg[:, dc, fc * 128:(fc + 1) * 128], hT[:, dc, :], dc == 0, dc == 15, (sgB, *hTB), (bgB,))
                        bu, buB = pp.get()
                        for dc in range(16):
                            P.mm(bu, su[:, dc, fc * 128:(fc + 1) * 128], hT[:, dc, :], dc == 0, dc == 15, (suB, *hTB), (buB,))
                        ft, ftB = f32p.get()
                        P.act(ft, bg, AF.Silu, (bgB,), (ftB,))
                        P.tt("dve", actT[:, s * 4 + fc, :], ft, bu, ALU.mult, (ftB, buB), (bigB,))
                accb = [bank[b] for b in range(4, 8)]
                accB = [bankb[b] for b in range(4, 8)]
                nxt = mix_prep(k + 1) if k + 1 < NOC else iter(())
                for cg in range(4):
                    for fg in range(4):
                        sl, slB = SS.get(sidx); sidx += 1
                        next(nxt, None)
                        for jf in range(11):
                            f = fg * 11 + jf
                            for t in range(4):
                                P.mm(accb[t], actT[:, f, t * 128:(t + 1) * 128], sl[:, jf, :], f == 0, f == NFC - 1,
                                     (bigB, slB), (accB[t],), inc=(f == NFC - 1) or (jf == 10 and t == 3))
                    for t in range(4):
                        xs = xres[t][:, cg * 512:(cg + 1) * 512]
                        P.tt("dve", xs, accb[t], xs, ALU.add, (accB[t], xrB[t]), (xrB[t],))
                for _ in nxt:
                    pass
                for t in range(4):
                    P.dma("sp", j["y"][c0 + t * 128:c0 + (t + 1) * 128, :], xres[t], (xrB[t],), ())
            P.barrier()
        yield

    gens_ = [job_gen(*jb_) for jb_ in jobs]
    while gens_:
        for g_ in list(gens_):
            try:
                next(g_)
            except StopIteration:
                gens_.remove(g_)
    P.barrier()
    with nc.Block() as block:
        @block.tensor
        def _(e):
            P.replay("pe", e)

        @block.scalar
        def _(e):
            P.replay("act", e)

        @block.vector
        def _(e):
            P.replay("dve", e)

        @block.gpsimd
        def _(e):
            P.replay("pool", e)

        @block.sync
        def _(e):
            P.replay("sp", e)
    return nc, P


def _pack_params(inp):
    p = np.zeros((128, NPAR), np.float32)

    def put(name, arr):
        p[:arr.shape[0], PC[name]:PC[name] + arr.shape[1]] = arr
    f = lambda a: np.asarray(a, np.float32)
    put("g_attn", f(inp["attn_norm_g"])[0].reshape(16, 128).T)
    put("g_ffn", f(inp["ffn_norm_g"])[0].reshape(16, 128).T)
    put("g_qa", f(inp["q_a_norm_g"])[0].reshape(4, 128).T)
    put("g_kva", f(inp["kv_a_norm_g"])[0].reshape(2, 128).T)
    sw = (np.arange(64) + 32) % 64
    gq = f(inp["q_norm_g"])[0]
    gk = f(inp["k_norm_g"])[0]
    put("g_qn", gq[:128, None])
    put("g_qp", gq[128:, None])
    put("g_qps", gq[128:][sw][:, None])
    put("g_kn", gk[:128, None])
    put("g_kp", gk[128:, None])
    put("g_kps", gk[128:][sw][:, None])
    cw = f(inp["conv_w"])[0]
    put("conv_w", cw.reshape(4, 8, 128).transpose(2, 1, 0).reshape(128, 32))
    put("conv_b", f(inp["conv_b"])[0].reshape(8, 128).T)
    put("b_r", f(inp["b_rg_r"])[0].reshape(16, 128).T)
    put("b_i", f(inp["b_rg_i"])[0].reshape(16, 128).T)
    put("lam", f(inp["lru_lambda"])[0].reshape(16, 128).T)
    put("g_ao", f(inp["attn_out_norm_g"])[0].reshape(8, 128).T)
    put("g_lo", f(inp["lru_out_norm_g"])[0].reshape(8, 128).T)
    return p


def _rope_table(pos):
    half = 32
    inv_freq = (np.float32(10000.0) ** (np.float32(-2.0) * np.arange(half, dtype=np.float32) / np.float32(64))).astype(np.float32)
    ang = (pos.astype(np.float32)[:, None] * inv_freq[None, :]).astype(np.float32)
    c = np.cos(ang.astype(np.float64)).astype(np.float32).T
    s = np.sin(ang.astype(np.float64)).astype(np.float32).T
    cosT = np.concatenate([c, c], 0)
    sinT = np.concatenate([-s, s], 0)
    return np.ascontiguousarray(np.stack([cosT, sinT], 0))


_CACHE = {}


def kernel(**inputs):
    f = lambda a: np.ascontiguousarray(np.asarray(a, np.float32))
    xp = f(inputs["x_prompt"])
    xs = f(inputs["x_sample"])
    cfg = {"jobs": [("p", 16384, 4096), ("s", 2048, 1024)], "debug": False}
    if "nc" not in _CACHE:
        _CACHE["nc"] = build(cfg)
    nc, P = _CACHE["nc"]
    params = _pack_params(inputs)
    sw = (np.arange(64) + 32) % 64
    w_in = f(inputs["w_in"])[0]
    w_pesw = np.ascontiguousarray(w_in[:, 768:832][:, sw])
    w_uq = f(inputs["w_uq"])[0]
    w_uqsw = np.ascontiguousarray(w_uq.reshape(512, 8, 192)[:, :, 128:][:, :, sw].reshape(512, 512))
    w_rg = np.ascontiguousarray(np.stack([f(inputs["w_rg_r"])[0], f(inputs["w_rg_i"])[0]], 0))
    common = {
        "w_in": w_in, "w_pesw": w_pesw, "w_uq": w_uq, "w_uqsw": w_uqsw, "w_ukv": f(inputs["w_ukv"])[0],
        "w_rg": w_rg, "w_out": f(inputs["w_out"])[0], "w_gate": f(inputs["w_gate"])[0], "w_up": f(inputs["w_up"])[0],
        "w_down": f(inputs["w_down"])[0], "ident": np.eye(128, dtype=np.float32), "params": params,
    }
    in_maps = []
    for c in range(8):
        pb, pq = c // 4, c % 4
        sb, sh = c // 2, c % 2
        m = dict(common)
        m["xp"] = np.ascontiguousarray(np.roll(xp[pb], -pq * 4096, axis=0))
        m["xs"] = np.ascontiguousarray(np.roll(xs[sb], -sh * 1024, axis=0))
        m["ropep"] = _rope_table((np.arange(16384) + pq * 4096) % 16384)
        m["ropes"] = _rope_table((np.arange(2048) + sh * 1024) % 2048)
        mk = np.ones((128, 8), np.float32)
        mk[:, (4 - pq) - 1] = 0.0
        mk[:, 4 + (2 - sh) - 1] = 0.0
        m["msk"] = mk
        in_maps.append(m)
    res = run_bass_kernel_spmd(nc, in_maps, core_ids=list(range(8)))
    yp = np.zeros((2, 16384, D), np.float32)
    ys = np.zeros((4, 2048, D), np.float32)
    for c in range(8):
        pb, pq = c // 4, c % 4
        sb, sh = c // 2, c % 2
        yp[pb, pq * 4096:(pq + 1) * 4096] = res.results[c]["yp"]
        ys[sb, sh * 1024:(sh + 1) * 1024] = res.results[c]["ys"]
    return (yp, ys)
```
